# Optimizing a Trainium2 kernel written in Bass

```python
import jax, jax.numpy as jnp
from jax import lax
import numpy as np

D_MODEL = 1024
BATCH = 2
SEQ = 8192
DEPTH = 4
DEC_BATCH = 128
DEC_SEQ = 8
PAST_LEN = 8192
PAGE_SIZE = 128

HEAD_DIM = 64
MIX_WIDTH = D_MODEL
A_WIDTH = MIX_WIDTH // 2
B_WIDTH = MIX_WIDTH - A_WIDTH
A_HEADS = A_WIDTH // HEAD_DIM
N_HEADS = B_WIDTH // HEAD_DIM
KV_HEADS = 2
GQA_GROUP = N_HEADS // KV_HEADS
CHUNK = 128
WINDOW = 128
BLOCK = WINDOW
ROPE_THETA = 500000.0
ROT_DIM = HEAD_DIM // 4
D_FF = 4 * D_MODEL
EPS = 1e-6
Q_COLS = N_HEADS * HEAD_DIM
KV_COLS = KV_HEADS * HEAD_DIM
IN_COLS = 2 * A_WIDTH + Q_COLS + 2 * KV_COLS
SPLITS = (A_WIDTH, 2 * A_WIDTH, 2 * A_WIDTH + Q_COLS, 2 * A_WIDTH + Q_COLS + KV_COLS)

kernel_name = "hymba_chunkgmlp_swa_sink_decoder_step"


def rms_norm(x, g):
    xf = x.astype(jnp.float32)
    y = xf * lax.rsqrt(jnp.mean(xf * xf, axis=-1, keepdims=True) + EPS) * g.astype(jnp.float32)
    return y.astype(x.dtype)


def rope(x, pos):
    half = ROT_DIM // 2
    inv = jnp.power(jnp.float32(ROPE_THETA), -2.0 * jnp.arange(half, dtype=jnp.float32) / ROT_DIM)
    ang = pos.astype(jnp.float32)[:, None] * inv[None, :]
    cos = jnp.cos(ang)[:, None, :]
    sin = jnp.sin(ang)[:, None, :]
    xf = x[..., :ROT_DIM].astype(jnp.float32)
    x1, x2 = xf[..., :half], xf[..., half:]
    rot = jnp.concatenate([x1 * cos - x2 * sin, x2 * cos + x1 * sin], axis=-1).astype(x.dtype)
    return jnp.concatenate([rot, x[..., ROT_DIM:]], axis=-1)


def mixer_inputs(x, pos, g_mix, w_in, g_sv, g_q, g_k):
    lead = x.shape[:-1]
    z = rms_norm(x, g_mix) @ w_in
    z_u, z_v, z_q, z_k, z_kv = jnp.split(z, SPLITS, axis=-1)
    u = jax.nn.gelu(z_u, approximate=False)
    va = rms_norm(jax.nn.gelu(z_v, approximate=False), g_sv).reshape(*lead, A_HEADS, HEAD_DIM)
    q = rope(rms_norm(z_q.reshape(*lead, N_HEADS, HEAD_DIM), g_q), pos)
    k = rope(rms_norm(z_k.reshape(*lead, KV_HEADS, HEAD_DIM), g_k), pos)
    v = z_kv.reshape(*lead, KV_HEADS, HEAD_DIM)
    return u, va, q, k, v


def attn_core(q, k_ctx, v_ctx, mask, sinks):
    qg = q.reshape(*q.shape[:-2], KV_HEADS, GQA_GROUP, HEAD_DIM)
    s = jnp.einsum('...qkgd,...jkd->...kgqj', qg, k_ctx,
                   preferred_element_type=jnp.float32) * (HEAD_DIM ** -0.5)
    s = jnp.where(mask, s, jnp.float32(-1e30))
    sink = jnp.broadcast_to(sinks.astype(jnp.float32).reshape(KV_HEADS, GQA_GROUP, 1, 1),
                            s.shape[:-1] + (1,))
    p = jax.nn.softmax(jnp.concatenate([s, sink], axis=-1), axis=-1)[..., :-1]
    o = jnp.einsum('...kgqj,...jkd->...qkgd', p.astype(v_ctx.dtype), v_ctx)
    return o.reshape(*o.shape[:-3], N_HEADS * HEAD_DIM)


def with_prev_block(t):
    prev = jnp.concatenate([jnp.zeros_like(t[:, :1]), t[:, :-1]], axis=1)
    return jnp.concatenate([prev, t], axis=2)


def merge_and_ffn(x, ya, yb, g_oa, g_ob, w_out, g_ffn, w_up, w_down):
    o = jnp.concatenate([rms_norm(ya, g_oa), rms_norm(yb, g_ob)], axis=-1) @ w_out
    x = x + o
    h = rms_norm(x, g_ffn) @ w_up
    return x + jnp.square(jax.nn.relu(h)) @ w_down


def setup_inputs(seed: int = 0) -> dict:
    key = jax.random.key(seed)
    ks = jax.random.split(key, 20)
    f32 = jnp.float32
    nrm = lambda k, s: jax.random.normal(k, s, f32)
    return {
        "x_prompt": nrm(ks[0], (BATCH, SEQ, D_MODEL)),
        "x_sample": nrm(ks[1], (DEC_BATCH, DEC_SEQ, D_MODEL)),
        "cache_win_k": nrm(ks[2], (DEPTH, DEC_BATCH, WINDOW, KV_HEADS, HEAD_DIM)),
        "cache_win_v": nrm(ks[3], (DEPTH, DEC_BATCH, WINDOW, KV_HEADS, HEAD_DIM)),
        "g_mix": 1.0 + 0.05 * nrm(ks[4], (DEPTH, D_MODEL)),
        "w_in": nrm(ks[5], (DEPTH, D_MODEL, IN_COLS)) * D_MODEL ** -0.5,
        "g_sv": 1.0 + 0.05 * nrm(ks[6], (DEPTH, A_WIDTH)),
        "g_q": 1.0 + 0.05 * nrm(ks[7], (DEPTH, HEAD_DIM)),
        "g_k": 1.0 + 0.05 * nrm(ks[8], (DEPTH, HEAD_DIM)),
        "w_spatial": nrm(ks[9], (DEPTH, A_HEADS, CHUNK, CHUNK)) * CHUNK ** -0.5,
        "b_spatial": 1.0 + 0.1 * nrm(ks[10], (DEPTH, A_HEADS, CHUNK)),
        "sinks": 0.5 * nrm(ks[11], (DEPTH, N_HEADS)),
        "g_out_a": 1.0 + 0.05 * nrm(ks[12], (DEPTH, A_WIDTH)),
        "g_out_b": 1.0 + 0.05 * nrm(ks[13], (DEPTH, B_WIDTH)),
        "w_out": nrm(ks[14], (DEPTH, MIX_WIDTH, D_MODEL)) * MIX_WIDTH ** -0.5,
        "g_ffn": 1.0 + 0.05 * nrm(ks[15], (DEPTH, D_MODEL)),
        "w_up": nrm(ks[16], (DEPTH, D_MODEL, D_FF)) * D_MODEL ** -0.5,
        "w_down": nrm(ks[17], (DEPTH, D_FF, D_MODEL)) * (0.5 * D_FF ** -0.5),
    }


def reference(x_prompt, x_sample, cache_win_k, cache_win_v, g_mix, w_in, g_sv, g_q, g_k,
              w_spatial, b_spatial, sinks, g_out_a, g_out_b, w_out, g_ffn, w_up, w_down):
    n_blocks = SEQ // BLOCK
    n_chunks = SEQ // CHUNK
    pos_p = jnp.arange(SEQ, dtype=jnp.int32)
    pos_s = PAST_LEN + jnp.arange(DEC_SEQ, dtype=jnp.int32)
    tril = jnp.tril(jnp.ones((CHUNK, CHUNK), dtype=bool))

    qi = jnp.arange(BLOCK)[:, None]
    kj = jnp.arange(2 * BLOCK)[None, :]
    dist = qi + BLOCK - kj
    band = (dist >= 0) & (dist <= WINDOW)
    key_pos = (jnp.arange(n_blocks)[:, None] - 1) * BLOCK + jnp.arange(2 * BLOCK)[None, :]
    mask_p = (band[None] & (key_pos >= 0)[:, None, :])[:, None, None]
    si = jnp.arange(DEC_SEQ)[:, None]
    sj = jnp.arange(WINDOW + DEC_SEQ)[None, :]
    sd = si + WINDOW - sj
    mask_s = (sd >= 0) & (sd <= WINDOW)

    xp, xs = x_prompt, x_sample
    nk_p, nv_p, nk_s, nv_s, nchunk_v = [], [], [], [], []
    for l in range(DEPTH):
        ws = jnp.where(tril, w_spatial[l], jnp.zeros_like(w_spatial[l]))
        bias_tc = b_spatial[l].T[:, :, None]

        u, va, q, k, v = mixer_inputs(xp, pos_p, g_mix[l], w_in[l], g_sv[l], g_q[l], g_k[l])
        vch = va.reshape(BATCH, n_chunks, CHUNK, A_HEADS, HEAD_DIM)
        ya = jnp.einsum('hts,bnshd->bnthd', ws, vch) + bias_tc
        ya = u * ya.reshape(BATCH, SEQ, A_WIDTH)
        kb = k.reshape(BATCH, n_blocks, BLOCK, KV_HEADS, HEAD_DIM)
        vb = v.reshape(BATCH, n_blocks, BLOCK, KV_HEADS, HEAD_DIM)
        qb = q.reshape(BATCH, n_blocks, BLOCK, N_HEADS, HEAD_DIM)
        yb = attn_core(qb, with_prev_block(kb), with_prev_block(vb), mask_p, sinks[l])
        yb = yb.reshape(BATCH, SEQ, B_WIDTH)
        nk_p.append(k[:, SEQ - WINDOW:])
        nv_p.append(v[:, SEQ - WINDOW:])
        xp = merge_and_ffn(xp, ya, yb, g_out_a[l], g_out_b[l], w_out[l], g_ffn[l], w_up[l], w_down[l])

        u, va, q, k, v = mixer_inputs(xs, pos_s, g_mix[l], w_in[l], g_sv[l], g_q[l], g_k[l])
        ya = jnp.einsum('hts,bshd->bthd', ws[:, :DEC_SEQ, :DEC_SEQ], va) + bias_tc[:DEC_SEQ]
        ya = u * ya.reshape(DEC_BATCH, DEC_SEQ, A_WIDTH)
        nchunk_v.append(va)
        k_ctx = jnp.concatenate([cache_win_k[l].astype(k.dtype), k], axis=1)
        v_ctx = jnp.concatenate([cache_win_v[l].astype(v.dtype), v], axis=1)
        yb = attn_core(q, k_ctx, v_ctx, mask_s, sinks[l]).reshape(DEC_BATCH, DEC_SEQ, B_WIDTH)
        nk_s.append(k_ctx[:, DEC_SEQ:])
        nv_s.append(v_ctx[:, DEC_SEQ:])
        xs = merge_and_ffn(xs, ya, yb, g_out_a[l], g_out_b[l], w_out[l], g_ffn[l], w_up[l], w_down[l])

    state_win_k_prompt = jnp.stack(nk_p)
    state_win_v_prompt = jnp.stack(nv_p)
    state_win_k_sample = jnp.stack(nk_s)
    state_win_v_sample = jnp.stack(nv_s)
    state_chunk_v_sample = jnp.stack(nchunk_v)
    return (xp, xs, state_win_k_prompt, state_win_v_prompt, state_win_k_sample, state_win_v_sample, state_chunk_v_sample)
```

```python
import types
import numpy as np
from contextlib import ExitStack
import concourse.bass as bass
import concourse.mybir as mybir
from concourse.bass_utils import run_bass_kernel_spmd

F32 = mybir.dt.float32
BF16 = mybir.dt.bfloat16
AF = mybir.ActivationFunctionType
ALU = mybir.AluOpType
AX = mybir.AxisListType

D = 1024
HD = 64
NH = 8
KVH = 2
G = 4
AW = 512
DFF = 4096
INC = 1792
EPS = 1e-6
NEG = -240000.0
WIN = 128
DEC = 8
NSEQ = 16
PAST = 8192
THETA = 500000.0
NSLOT = 8


def _freeze(fn):
    if fn.__closure__ is None:
        return fn
    cells = tuple(types.CellType(c.cell_contents) for c in fn.__closure__)
    return types.FunctionType(fn.__code__, fn.__globals__, fn.__name__, fn.__defaults__, cells)


class Buf:
    def __init__(self, t, name):
        self.t = t
        self.name = name
        self.w = None
        self.r = {}
        self.dsem = None
        self.dcnt = 0

    def __getitem__(self, k):
        return self.t[k]


class Sched:
    def __init__(self, nc, es):
        self.nc = nc
        self.es = es
        self.eng = {"pe": nc.tensor, "act": nc.scalar, "dve": nc.vector, "pool": nc.gpsimd, "sp": nc.sync}
        self.sems = {}
        self.cnt = {}
        self.known = {e: {} for e in self.eng}
        for e in self.eng:
            self.sems[e] = es.enter_context(nc.semaphore("sem_" + e))
            self.cnt[e] = 0
        self.final = {}
        self.nwaits = 0
        self.rec = None

    def buf(self, shape, dtype, name):
        t = self.es.enter_context(self.nc.sbuf_tensor(name, list(shape), dtype))
        return Buf(t, name)

    def _deps(self, e, R, W, dmabuf=None):
        need = {}
        for b in R:
            if b.w is not None:
                k, v = b.w
                need[k] = max(need.get(k, 0), v)
        for b in W:
            if b.w is not None:
                k, v = b.w
                if not (dmabuf is b and k == b.dsem):
                    need[k] = max(need.get(k, 0), v)
            for k, v in b.r.items():
                need[k] = max(need.get(k, 0), v)
        kn = self.known[e]
        for k, v in need.items():
            if k == "pe" and e == "pe":
                continue
            if kn.get(k, 0) >= v:
                continue
            self.eng[e].wait_ge(self.sems[k], v)
            self.nwaits += 1
            kn[k] = v

    def _mark(self, tok, R, W):
        k, v = tok
        for b in R:
            b.r[k] = max(b.r.get(k, 0), v)
        for b in W:
            b.w = tok
            b.r = {}

    def op(self, e, fn, R=(), W=()):
        if self.rec is not None:
            self.rec.append(("op", e, _freeze(fn), tuple(R), tuple(W)))
            return
        self._deps(e, R, W)
        ins = fn(self.eng[e])
        self.cnt[e] += 1
        ins.then_inc(self.sems[e], 1)
        self._mark((e, self.cnt[e]), R, W)

    def replay(self, it):
        if it[0] == "op":
            self.op(it[1], it[2], it[3], it[4])
        else:
            self.dma(it[1], it[2], it[3], it[4], it[5], it[6], **it[7])

    def record(self, gen):
        self.rec = []
        if gen is not None:
            next(gen, None)
        r = self.rec
        self.rec = None
        return r

    def emit_merged(self, a, b):
        import os
        if os.environ.get("NOZIP"):
            for it in list(a) + list(b):
                self.replay(it)
            return
        na, nb = len(a), len(b)
        i = j = 0
        while i < na or j < nb:
            if j >= nb or (i < na and i * nb <= j * na):
                self.replay(a[i])
                i += 1
            else:
                self.replay(b[j])
                j += 1

    def dma(self, e, out, in_, R=(), W=(), final=False, **kw):
        if self.rec is not None:
            self.rec.append(("dma", e, out, in_, tuple(R), tuple(W), final, kw))
            return
        sb = None
        for b in list(W) + list(R):
            sb = b
            break
        if sb is None:
            key = "dram"
            if key not in self.sems:
                self.sems[key] = self.es.enter_context(self.nc.semaphore("sem_dram"))
                self.cnt[key] = 0
            self._deps(e, R, W)
            self.cnt[key] += 16
            self.eng[e].dma_start(out=out, in_=in_, **kw).then_inc(self.sems[key], 16)
            self.final[key] = self.cnt[key]
            return
        if sb.dsem is None:
            sb.dsem = "d_" + sb.name
            self.sems[sb.dsem] = self.es.enter_context(self.nc.semaphore("sem_" + sb.dsem))
        self._deps(e, R, W, dmabuf=sb)
        sb.dcnt += 16
        self.eng[e].dma_start(out=out, in_=in_, **kw).then_inc(self.sems[sb.dsem], 16)
        tok = (sb.dsem, sb.dcnt)
        self._mark(tok, R, W)
        if final:
            self.final[sb.dsem] = sb.dcnt

    def barrier(self, bufs):
        toks = {e: self.cnt[e] for e in ("pe", "act", "dve", "pool")}
        for b in bufs:
            if b.w is not None:
                toks[b.w[0]] = max(toks.get(b.w[0], 0), b.w[1])
            for k, v in b.r.items():
                toks[k] = max(toks.get(k, 0), v)
        for e in ("pe", "act", "dve", "pool", "sp"):
            kn = self.known[e]
            for k, v in toks.items():
                if v > 0 and kn.get(k, 0) < v:
                    self.eng[e].wait_ge(self.sems[k], v)
                    kn[k] = v
        for b in bufs:
            b.w = None
            b.r = {}

    def finish(self):
        for k, v in self.final.items():
            self.nc.sync.wait_ge(self.sems[k], v)


def build(L=4, NOWN=16, dbg=None):
    H = L
    NB = H + NOWN + 1
    SB = NB - 1
    nc = bass.Bass("TRN2", target_bir_lowering=False)
    dt = nc.dram_tensor
    xin = dt("xin", [NB, 128, D], F32, kind="ExternalInput").ap()
    ck = dt("ck", [L, NSEQ, WIN, 128], F32, kind="ExternalInput").ap()
    cv = dt("cv", [L, NSEQ, WIN, 128], F32, kind="ExternalInput").ap()
    w_in = dt("w_in", [L, D, INC], F32, kind="ExternalInput").ap()
    w_out = dt("w_out", [L, D, D], F32, kind="ExternalInput").ap()
    w_up = dt("w_up", [L, D, DFF], F32, kind="ExternalInput").ap()
    w_down = dt("w_down", [L, DFF, D], F32, kind="ExternalInput").ap()
    wsT = dt("wsT", [L, 128, NH * 128], F32, kind="ExternalInput").ap()
    gT = dt("gT", [L, 128, 24], F32, kind="ExternalInput").ap()
    gsv = dt("gsv", [L, AW], F32, kind="ExternalInput").ap()
    gqk = dt("gqk", [L, 128], F32, kind="ExternalInput").ap()
    bsp = dt("bsp", [L, 128, NH], F32, kind="ExternalInput").ap()
    snk = dt("snk", [L, NH], F32, kind="ExternalInput").ap()
    cs = dt("cs", [128, NB * 16], F32, kind="ExternalInput").ap()
    masks = dt("masks", [128, 5 * 128], F32, kind="ExternalInput").ap()
    tril = dt("tril", [128, 256], F32, kind="ExternalInput").ap()
    wsTS = dt("wsTS", [L, 128, NH * 128], F32, kind="ExternalInput").ap()
    bspS = dt("bspS", [L, 128, NH], F32, kind="ExternalInput").ap()
    identf = dt("identf", [128, 128], F32, kind="ExternalInput").ap()
    y = dt("y", [NOWN + 1, 128, D], F32, kind="ExternalOutput").ap()
    kp = dt("kp", [L, 128, 128], F32, kind="ExternalOutput").ap()
    vp = dt("vp", [L, 128, 128], F32, kind="ExternalOutput").ap()
    ks = dt("ks", [L, NSEQ, WIN, 128], F32, kind="ExternalOutput").ap()
    vs = dt("vs", [L, NSEQ, WIN, 128], F32, kind="ExternalOutput").ap()
    cvs = dt("cvs", [L, 128, AW], F32, kind="ExternalOutput").ap()

    es = ExitStack()
    S = Sched(nc, es)
    B = S.buf
    X = [B([128, D], F32, f"x{i}") for i in range(NB)]
    SLOT = [B([128, 4096], BF16, f"slot{i}") for i in range(NSLOT)]
    NSTG = 2
    STG = [B([128, 1024], F32, f"stg{i}") for i in range(NSTG)]
    NBF = NB - 1
    ARENA_BYTES = max(NBF * 2048, 40960)
    ARENA = B([128, ARENA_BYTES // 2], BF16, "arena")
    aoff = [0]

    def carve(shape, dtype, name):
        nel = int(np.prod(shape[1:]))
        esz = 4 if dtype == F32 else 2
        nbytes = (nel * esz + 31) // 32 * 32
        o = aoff[0]
        aoff[0] += nbytes
        assert aoff[0] <= ARENA_BYTES, (name, aoff[0])
        ap = ARENA.t[:, o // 2:(o + nel * esz) // 2]
        if dtype == F32:
            ap = ap.bitcast(F32)
        if len(shape) == 3:
            ap = ap.rearrange("p (a b) -> p a b", a=shape[1])
        elif len(shape) == 4:
            ap = ap.rearrange("p (a b c) -> p a b c", a=shape[1], b=shape[2])
        return Buf(ap, name)

    Cv = carve
    STATS = B([128, 224], F32, "stats")
    soff = [0]

    def small(n, name):
        o = soff[0]
        soff[0] += n
        assert soff[0] <= 224, name
        return Buf(STATS.t[:, o:o + n], name)

    JUNKT = B([128, 8], BF16, "junk")
    JK = [Buf(JUNKT.t[:, i:i + 1], f"junk{i}") for i in range(5)]
    XB = Cv([128, D], BF16, "xb")
    CAT = Cv([128, D], BF16, "cat")
    XT = [Cv([128, 8, 128], BF16, f"xt{i}") for i in range(2)]
    YA = Cv([128, 512], F32, "ya")
    YB = Cv([128, 512], F32, "yb")
    QK = Cv([128, 640], F32, "qk")
    QKB = Cv([128, 640], BF16, "qkb")
    QT = Cv([128, 512], BF16, "qt")
    KT = [Cv([128, 128], BF16, f"kt{i}") for i in range(3)]
    V65 = [Cv([128, KVH, 65], BF16, f"v65_{i}") for i in range(3)]
    PT23 = [Cv([128, 512], BF16, f"pt{i}") for i in (2, 3)]
    OT = Cv([128, 1024], F32, "ot")
    VF = Cv([128, 128], F32, "vf")
    VA = Cv([128, 512], BF16, "va")
    CKB = XB
    CV65 = Cv([128, NSEQ, KVH, 65], BF16, "cv65")
    GSV = Cv([128, AW], F32, "gsvb")
    WST = Cv([128, NH * 128], BF16, "wst")
    GQKB = Cv([128, 128], F32, "gqkb")
    QT2 = Cv([128, 512], BF16, "qt2")
    VA2 = Cv([128, 512], BF16, "va2")
    MSK = Cv([128, 5 * 128], BF16, "mskb")
    CS = Cv([128, NB * 16], F32, "csb")
    TRI = Cv([128, 256], F32, "trib")
    MIXBUFS = [XB, CAT] + XT + [YA, YB, QK, QKB, QT] + KT + V65 + PT23 + [OT, VF, VA, CV65, GSV, WST, GQKB, CS, TRI, QT2, VA2, MSK]
    XTALL = {}
    for i_ in range(NBF):
        XTALL[i_] = Buf(ARENA.t[:, i_ * 1024:(i_ + 1) * 1024].rearrange("p (k t) -> p k t", k=8), f"xtall{i_}")
    U = B([128, 512], F32, "u")
    GV = B([128, 512], F32, "gv")
    PT = [B([128, 512], BF16, f"pt{i}") for i in range(2)] + PT23
    U2 = B([128, 512], F32, "u2")
    HT2 = [B([128, 512], BF16, f"ht2_{i}") for i in range(2)]
    UL, VAL, QTL = [U, U2], [VA, VA2], [QT, QT2]
    HTL = [[PT[0], PT[1]], HT2]
    mixn = [0]
    ffnn = [0]
    BSP = B([128, NH], F32, "bspb")
    ESK = B([128, NH], F32, "esk")
    GT = B([128, 24], F32, "gtb")
    IDB = B([128, 128], BF16, "idb")
    IDF = B([128, 128], F32, "idf")
    EPSC = small(1, "epsc")
    SSX = small(NB, "ssx")
    VX = small(NB, "vx")
    RSX = small(NB, "rsx")
    EPQ = small(NB, "epq")
    SS2 = small(NB, "ss2x")
    R2Q = small(NB, "r2q")
    SSB = small(12, "ssb")
    V11 = small(12, "v11")
    R11 = small(12, "r11")
    SSO = small(2, "sso")
    VO = small(2, "vo")
    RO = small(2, "ro")
    DEN = small(NH, "den")
    RDEN = small(NH, "rden")
    PSt = es.enter_context(nc.psum_tensor("ps", [128, 8, 512], F32))
    P = [Buf(None, f"ps{i}") for i in range(8)]
    pools = {"all": [list(range(8)), 0], "A": [[0, 1, 2, 3], 0], "B": [[4, 5, 6, 7], 0]}
    cur_pool = ["all"]

    def bank():
        p = pools[cur_pool[0]]
        i = p[0][p[1] % len(p[0])]
        p[1] += 1
        return i

    def bank2():
        p = pools[cur_pool[0]]
        if p[1] % 2:
            p[1] += 1
        i = p[0][p[1] % len(p[0])]
        p[1] += 2
        return i

    def pf(i, lo=0, hi=512):
        return PSt[:, i, lo:hi]

    def pb(i):
        return PSt[:, i, :].bitcast(BF16)

    def p2(i):
        return PSt[:, i:i + 2, :].rearrange("p a b -> p (a b)")

    S.dma("sp", IDF[:, :], identf[:, :], W=[IDF])
    S.dma("pool", IDB[:, :], identf[:, :], W=[IDB])
    S.op("dve", lambda e: e.memset(EPSC[:, :], EPS), W=[EPSC])
    S.op("dve", lambda e: e.memset(SSX[:, :], 1.0), W=[SSX])
    S.op("dve", lambda e: e.memset(SS2[:, :], 1.0), W=[SS2])
    def order(l):
        return list(range(l, H)) + list(range(H, H + NOWN)) + [SB]

    for b in order(0)[:3]:
        S.dma("act", X[b][:, :], xin[b, :, :], W=[X[b]])

    pieces = []
    for l in range(L):
        for j in range(4):
            pieces.append(("in", l, j))
        for j in range(2):
            pieces.append(("out", l, j))
        for c in range(4):
            pieces += [("up", l, c, 0), ("up", l, c, 1), ("dn", l, c, 0), ("dn", l, c, 1)]
    st = {"next": 0, "released": 0, "stg": 0, "gl": -1}
    pidx = {p: i for i, p in enumerate(pieces)}

    def slot_of(p):
        return SLOT[pidx[p] % NSLOT]

    def load_gains(l):
        if st["gl"] == l:
            return
        st["gl"] = l
        S.dma("sp", GT[:, :], gT[l, :, :], W=[GT])

    def staged(slot, src_fn, ncols, goff):
        sv = slot[:, :].rearrange("p (k c) -> p k c", k=8)
        kstep = 1024 // ncols
        for k0 in range(0, 8, kstep):
            sg = STG[st["stg"] % NSTG]
            st["stg"] += 1
            sgv = sg[:, :].rearrange("p (k c) -> p k c", k=kstep)
            S.dma("sp", sgv, src_fn(k0, k0 + kstep), W=[sg])
            S.op("pool", lambda e, sgv=sgv, k0=k0: e.tensor_tensor(
                out=sv[:, k0:k0 + kstep, 0:ncols], in0=sgv,
                in1=GT[:, goff + k0:goff + k0 + kstep].unsqueeze(2).to_broadcast([128, kstep, ncols]), op=ALU.mult),
                R=[sg, GT], W=[slot])

    def load_piece(p):
        slot = slot_of(p)
        kind, l = p[0], p[1]
        if kind == "in":
            load_gains(l)
            j = p[2]
            n = 512 if j < 3 else 256
            staged(slot, lambda a, b_, l=l, j=j, n=n: w_in[l, a * 128:b_ * 128, j * 512:j * 512 + n].rearrange("(k p) c -> p k c", p=128), n, 0)
        elif kind == "out":
            j = p[2]
            staged(slot, lambda a, b_, l=l, j=j: w_out[l, a * 128:b_ * 128, j * 512:(j + 1) * 512].rearrange("(k p) c -> p k c", p=128), 512, 8)
        elif kind == "up":
            c, j = p[2], p[3]
            base = c * 1024 + j * 512
            staged(slot, lambda a, b_, l=l, base=base: w_up[l, a * 128:b_ * 128, base:base + 512].rearrange("(k p) c -> p k c", p=128), 512, 16)
        else:
            c, j = p[2], p[3]
            r0 = c * 1024 + j * 512
            S.dma("pool", slot[:, :].rearrange("p (m c) -> p m c", m=4),
                  w_down[l, r0:r0 + 512, :].rearrange("(m p) c -> p m c", p=128), W=[slot])

    def load_spatial(l, src, toff):
        for hh in range(2):
            sg = STG[st["stg"] % NSTG]
            st["stg"] += 1
            S.dma("sp", sg[:, 0:512], src[l, :, hh * 512:(hh + 1) * 512], W=[sg])
            S.op("pool", lambda e, sg=sg, hh=hh: e.tensor_tensor(
                out=WST[:, hh * 512:(hh + 1) * 512].rearrange("p (h t) -> p h t", h=4),
                in0=sg[:, 0:512].rearrange("p (h t) -> p h t", h=4),
                in1=TRI[:, toff:toff + 128].unsqueeze(1).to_broadcast([128, 4, 128]), op=ALU.mult), R=[sg, TRI], W=[WST])

    def cache_prep(l):
        cur_pool[0] = "A"
        CKT = X[0]
        ckt = CKT[:, :].bitcast(BF16).rearrange("p (s t) -> p s t", s=NSEQ)
        for hh in range(2):
            for which, src in (("k", ck), ("v", cv)):
                sg = STG[st["stg"] % NSTG]
                st["stg"] += 1
                S.dma("sp", sg[:, :].rearrange("p (s c) -> p s c", s=8),
                      src[l, hh * 8:(hh + 1) * 8, :, :].rearrange("s k c -> k s c"), W=[sg])
                if which == "k":
                    for q4 in range(2):
                        i = bank()
                        for s_ in range(4):
                            sq = q4 * 4 + s_
                            S.op("pe", lambda e, s_=s_, sq=sq, i=i, sg=sg: e.transpose(
                                out=pf(i, s_ * 128, (s_ + 1) * 128), in_=sg[:, sq * 128:(sq + 1) * 128], identity=IDF[:, :]),
                                R=[sg, IDF], W=[P[i]])
                        S.op("act", lambda e, i=i, hh=hh, q4=q4: e.activation(
                            out=ckt[:, hh * 8 + q4 * 4:hh * 8 + q4 * 4 + 4, :].rearrange("p s t -> p (s t)"), in_=pf(i), func=AF.Copy),
                            R=[P[i]], W=[CKT])
                else:
                    S.op("pool", lambda e, sg=sg, hh=hh: e.tensor_copy(
                        out=CV65[:, hh * 8:(hh + 1) * 8, :, 0:HD],
                        in_=sg[:, :].rearrange("p (s k d) -> p s k d", s=8, k=KVH)), R=[sg], W=[CV65])

    def pump(limit=None):
        while st["next"] < len(pieces) and st["next"] < st["released"] + NSLOT and (limit is None or st["next"] < limit):
            load_piece(pieces[st["next"]])
            st["next"] += 1

    pump(limit=4)
    for b in order(0)[3:]:
        S.dma("act", X[b][:, :], xin[b, :, :], R=[slot_of(("in", 0, 3))], W=[X[b]])

    chain = {"kt": None, "v": None, "n": 0}

    def transposes8(src, dstT, ev="dve"):
        i = bank()
        for k in range(8):
            S.op("pe", lambda e, k=k: e.transpose(out=pb(i)[:, k * 128:(k + 1) * 128], in_=src[:, k * 128:(k + 1) * 128],
                                                 identity=IDB[:, :]), R=[src, IDB], W=[P[i]])
        if ev == "dve":
            S.op("dve", lambda e: e.tensor_copy(out=dstT[:, :, :].rearrange("p k t -> p (k t)"), in_=pb(i)), R=[P[i]], W=[dstT])
        else:
            S.op("act", lambda e: e.activation(out=dstT[:, :, :].rearrange("p k t -> p (k t)"), in_=pb(i), func=AF.Copy),
                 R=[P[i]], W=[dstT])

    def mixer_block(l, b, first, is_sample, mask_prev_idx):
        kvonly = first and not is_sample
        par = mixn[0] % 2
        mixn[0] += 1
        Ub, VAb, QTb = UL[par], VAL[par], QTL[par]
        cur_pool[0] = "A"
        emit_state = is_sample or (b == H + NOWN - 1)
        rs = RSX[:, b:b + 1]
        Sl = [slot_of(("in", l, j)) for j in range(4)]
        So = [slot_of(("out", l, j)) for j in range(2)]
        S.op("act", lambda e: e.activation(out=XB[:, :], in_=X[b][:, :], func=AF.Copy), R=[X[b]], W=[XB])
        xt = XT[0]
        transposes8(XB, xt)
        zb = [bank() for _ in range(4)]
        ncol = [512, 512, 512, 256]
        for k in range(8):
            for j in range(4):
                if kvonly and j < 2:
                    continue
                S.op("pe", lambda e, k=k, j=j: e.matmul(pf(zb[j], 0, ncol[j]), lhsT=xt[:, k, :],
                                                       rhs=Sl[j][:, :].rearrange("p (k c) -> p k c", k=8)[:, k, 0:ncol[j]],
                                                       start=(k == 0), stop=(k == 7)), R=[xt, Sl[j]], W=[P[zb[j]]])
        if not kvonly:
            S.op("act", lambda e: e.activation(out=Ub[:, :], in_=pf(zb[0]), func=AF.Gelu, scale=rs), R=[P[zb[0]], RSX], W=[Ub])
            S.op("act", lambda e: e.activation(out=GV[:, :], in_=pf(zb[1]), func=AF.Gelu, scale=rs), R=[P[zb[1]], RSX], W=[GV])
            S.op("act", lambda e: e.activation(out=JK[1][:, 0:1].to_broadcast([128, 512]), in_=GV[:, :], func=AF.Square, accum_out=SSB[:, 0:1]),
                 R=[GV], W=[SSB, JK[1]])
        else:
            S.op("dve", lambda e: e.memset(SSB[:, 0:1], 1.0), W=[SSB])
        S.op("act", lambda e: e.activation(out=pf(zb[0]), in_=pf(zb[2]), func=AF.Square), R=[P[zb[2]]], W=[P[zb[0]]])
        S.op("act", lambda e: e.activation(out=pf(zb[3], 256, 384), in_=pf(zb[3], 0, 128), func=AF.Square), R=[P[zb[3]]], W=[P[zb[3]]])
        S.op("dve", lambda e: e.tensor_reduce(out=SSB[:, 1:9], in_=pf(zb[0]).rearrange("p (h d) -> p h d", d=HD),
                                              axis=AX.X, op=ALU.add), R=[P[zb[0]]], W=[SSB])
        S.op("dve", lambda e: e.tensor_reduce(out=SSB[:, 9:11], in_=pf(zb[3], 256, 384).rearrange("p (h d) -> p h d", d=HD),
                                              axis=AX.X, op=ALU.add), R=[P[zb[3]]], W=[SSB])
        S.op("act", lambda e: e.activation(out=V11[:, 0:1], in_=SSB[:, 0:1], func=AF.Ln, bias=EPSC[:, 0:1], scale=1.0 / AW),
             R=[SSB, EPSC], W=[V11])
        S.op("act", lambda e: e.activation(out=V11[:, 1:11], in_=SSB[:, 1:11], func=AF.Ln, bias=EPQ[:, b:b + 1], scale=1.0 / HD),
             R=[SSB, EPQ], W=[V11])
        S.op("act", lambda e: e.activation(out=R11[:, 0:11], in_=V11[:, 0:11], func=AF.Exp, scale=-0.5), R=[V11], W=[R11])
        if not kvonly:
            S.op("dve", lambda e: e.scalar_tensor_tensor(out=VAb[:, :], in0=GV[:, :], scalar=R11[:, 0:1], in1=GSV[:, :],
                                                         op0=ALU.mult, op1=ALU.mult), R=[GV, R11, GSV], W=[VAb])
            if is_sample:
                S.op("dve", lambda e: e.scalar_tensor_tensor(out=GV[:, :], in0=GV[:, :], scalar=R11[:, 0:1], in1=GSV[:, :],
                                                             op0=ALU.mult, op1=ALU.mult), R=[GV, R11, GSV], W=[GV])
                S.dma("sp", cvs[l, :, :], GV[:, :], R=[GV], final=True)
        qk3 = QK[:, :].rearrange("p (h d) -> p h d", d=HD)
        S.op("dve", lambda e: e.tensor_tensor(out=qk3[:, 0:8, :], in0=pf(zb[2]).rearrange("p (h d) -> p h d", d=HD),
                                              in1=R11[:, 1:9].unsqueeze(2).to_broadcast([128, 8, HD]), op=ALU.mult),
             R=[P[zb[2]], R11], W=[QK])
        S.op("dve", lambda e: e.tensor_tensor(out=qk3[:, 8:10, :], in0=pf(zb[3], 0, 128).rearrange("p (h d) -> p h d", d=HD),
                                              in1=R11[:, 9:11].unsqueeze(2).to_broadcast([128, 2, HD]), op=ALU.mult),
             R=[P[zb[3]], R11], W=[QK])
        S.op("dve", lambda e: e.tensor_tensor(out=qk3[:, 0:8, :], in0=qk3[:, 0:8, :],
                                               in1=GQKB[:, 0:64].unsqueeze(1).to_broadcast([128, 8, HD]), op=ALU.mult),
             R=[QK, GQKB], W=[QK])
        S.op("dve", lambda e: e.tensor_tensor(out=qk3[:, 8:10, :], in0=qk3[:, 8:10, :],
                                               in1=GQKB[:, 64:128].unsqueeze(1).to_broadcast([128, 2, HD]), op=ALU.mult),
             R=[QK, GQKB], W=[QK])
        rt = GV[:, 0:320].rearrange("p (a h d) -> p a h d", a=4, d=8)
        cosb = CS[:, b * 16:b * 16 + 8].unsqueeze(1).to_broadcast([128, 10, 8])
        sinb = CS[:, b * 16 + 8:b * 16 + 16].unsqueeze(1).to_broadcast([128, 10, 8])
        x1 = qk3[:, :, 0:8]
        x2 = qk3[:, :, 8:16]
        for a, (xa, tb) in enumerate([(x1, cosb), (x2, sinb), (x2, cosb), (x1, sinb)]):
            S.op("dve", lambda e, a=a, xa=xa, tb=tb: e.tensor_tensor(out=rt[:, a, :, :], in0=xa, in1=tb, op=ALU.mult),
                 R=[QK, CS], W=[GV])
        S.op("dve", lambda e: e.tensor_tensor(out=x1, in0=rt[:, 0, :, :], in1=rt[:, 1, :, :], op=ALU.subtract), R=[GV], W=[QK])
        S.op("dve", lambda e: e.tensor_tensor(out=x2, in0=rt[:, 2, :, :], in1=rt[:, 3, :, :], op=ALU.add), R=[GV], W=[QK])
        S.op("dve", lambda e: e.tensor_copy(out=QKB[:, 0:512].rearrange("p (g kv d) -> p kv g d", g=G, kv=KVH),
                                             in_=QK[:, 0:512].rearrange("p (kv g d) -> p kv g d", kv=KVH, g=G)),
             R=[QK], W=[QKB])
        S.op("dve", lambda e: e.tensor_copy(out=QKB[:, 512:640], in_=QK[:, 512:640]), R=[QK], W=[QKB])
        n = chain["n"]
        vcur = V65[n % 3]
        ktcur = KT[n % 3]
        chain["n"] += 1
        S.op("dve", lambda e: e.tensor_scalar(out=vcur[:, :, 0:HD], in0=pf(zb[3], 128, 256).rearrange("p (k d) -> p k d", d=HD),
                                              scalar1=rs, scalar2=None, op0=ALU.mult), R=[P[zb[3]], RSX], W=[vcur])
        if emit_state:
            S.op("dve", lambda e: e.tensor_scalar(out=VF[:, :], in0=pf(zb[3], 128, 256), scalar1=rs, scalar2=None, op0=ALU.mult),
                 R=[P[zb[3]], RSX], W=[VF])
            if is_sample:
                for s in range(NSEQ):
                    S.dma("sp", ks[l, s, WIN - DEC:WIN, :], QK[s * DEC:(s + 1) * DEC, 512:640], R=[QK], final=True)
                    S.dma("sp", vs[l, s, WIN - DEC:WIN, :], VF[s * DEC:(s + 1) * DEC, :], R=[VF], final=True)
            else:
                S.dma("sp", kp[l, :, :], QK[:, 512:640], R=[QK], final=True)
                S.dma("sp", vp[l, :, :], VF[:, :], R=[VF], final=True)
        tb_ = bank()
        for g in range(5):
            S.op("pe", lambda e, g=g: e.transpose(out=pb(tb_)[:, g * 128:(g + 1) * 128], in_=QKB[:, g * 128:(g + 1) * 128],
                                                  identity=IDB[:, :]), R=[QKB, IDB], W=[P[tb_]])
        S.op("act", lambda e: e.activation(out=QTb[:, :], in_=pb(tb_)[:, 0:512], func=AF.Copy), R=[P[tb_]], W=[QTb])
        S.op("act", lambda e: e.activation(out=ktcur[:, :], in_=pb(tb_)[:, 512:640], func=AF.Copy), R=[P[tb_]], W=[ktcur])
        ktprev, vprev = chain["kt"], chain["v"]
        if not is_sample:
            chain["kt"], chain["v"] = ktcur, vcur
        if kvonly:
            return
        yield
        cur_pool[0] = "B"
        if is_sample:
            load_spatial(l, wsTS, 128)
            S.dma("sp", BSP[:, :], bspS[l, :, :], W=[BSP])
        qt3 = QTb[:, :].rearrange("p (g t) -> p g t", g=G)

        def maskmm(sb_, midx, start, stop):
            S.op("pe", lambda e: e.matmul(pf(sb_).rearrange("p (g t) -> p g t", g=G), lhsT=IDB[:, :],
                                          rhs=MSK[:, midx * 128:(midx + 1) * 128].unsqueeze(1).to_broadcast([128, G, 128]),
                                          start=start, stop=stop), R=[IDB, MSK], W=[P[sb_]])
        parts = []
        if is_sample:
            CKT = X[0]
            ckt = CKT[:, :].bitcast(BF16).rearrange("p (s t) -> p s t", s=NSEQ)
            for kv in range(KVH):
                sb_ = bank()
                maskmm(sb_, 4, True, False)
                for s in range(NSEQ):
                    for g in range(G):
                        S.op("pe", lambda e, kv=kv, s=s, g=g: e.matmul(
                            pf(sb_, g * 128 + s * DEC, g * 128 + (s + 1) * DEC),
                            lhsT=ckt[kv * 64:(kv + 1) * 64, s, :], rhs=qt3[kv * 64:(kv + 1) * 64, g, s * DEC:(s + 1) * DEC],
                            start=False, stop=(s == NSEQ - 1 and g == G - 1)), R=[CKT, QTb], W=[P[sb_]])
                S.op("act", lambda e, kv=kv: e.activation(out=PT[kv][:, :], in_=pf(sb_), func=AF.Exp, scale=0.125),
                     R=[P[sb_]], W=[PT[kv]])
            srcs = [(ktcur, 3)]
        else:
            srcs = [(ktprev, mask_prev_idx), (ktcur, 1)]
        for pi, (ktb, midx) in enumerate(srcs):
            for kv in range(KVH):
                sb_ = bank()
                S.op("pe", lambda e, kv=kv, ktb=ktb: e.matmul(pf(sb_), lhsT=ktb[kv * 64:(kv + 1) * 64, :],
                                                              rhs=QTb[kv * 64:(kv + 1) * 64, :], start=True, stop=False),
                     R=[ktb, QTb], W=[P[sb_]])
                maskmm(sb_, midx, False, True)
                pt = PT[2 + kv] if (is_sample or pi == 1) else PT[kv]
                S.op("act", lambda e, pt=pt: e.activation(out=pt[:, :], in_=pf(sb_), func=AF.Exp, scale=0.125),
                     R=[P[sb_]], W=[pt])
        if not is_sample:
            yb_ = [bank(), bank()]
            for kv in range(KVH):
                for g in range(G):
                    for pi, (ptb, vb) in enumerate(((PT[kv], vprev), (PT[2 + kv], vcur))):
                        S.op("pe", lambda e, kv=kv, g=g, pi=pi, ptb=ptb, vb=vb: e.matmul(
                            pf(yb_[kv], g * 65, (g + 1) * 65), lhsT=ptb[:, g * 128:(g + 1) * 128], rhs=vb[:, kv, :],
                            start=(pi == 0), stop=(pi == 1)), R=[ptb, vb], W=[P[yb_[kv]]])
        else:
            for kv in range(KVH):
                ob = bank()
                S.op("pe", lambda e, kv=kv: e.matmul(pf(ob)[0:65, :], lhsT=vcur[:, kv, :], rhs=PT[2 + kv][:, :],
                                                     start=True, stop=False), R=[vcur, PT[2 + kv]], W=[P[ob]])
                for s in range(NSEQ):
                    for g in range(G):
                        S.op("pe", lambda e, kv=kv, s=s, g=g: e.matmul(
                            pf(ob, g * 128 + s * DEC, g * 128 + (s + 1) * DEC)[0:65, :],
                            lhsT=CV65[:, s, kv, :], rhs=PT[kv][:, g * 128 + s * DEC:g * 128 + (s + 1) * DEC],
                            start=False, stop=(s == NSEQ - 1 and g == G - 1)), R=[CV65, PT[kv]], W=[P[ob]])
                S.op("act", lambda e, kv=kv: e.activation(out=OT[0:65, kv * 512:(kv + 1) * 512], in_=pf(ob)[0:65, :], func=AF.Copy),
                     R=[P[ob]], W=[OT])
            yb_ = [bank(), bank()]
            for kv in range(KVH):
                for g in range(G):
                    S.op("pe", lambda e, kv=kv, g=g: e.transpose(out=pf(yb_[kv], g * 65, (g + 1) * 65),
                                                                in_=OT[0:65, kv * 512 + g * 128:kv * 512 + (g + 1) * 128],
                                                                identity=IDF[0:65, 0:65]), R=[OT, IDF], W=[P[yb_[kv]]])
        for kv in range(KVH):
            o3 = pf(yb_[kv], 0, 260).rearrange("p (g d) -> p g d", d=65)
            S.op("dve", lambda e, kv=kv, o3=o3: e.tensor_tensor(out=DEN[:, kv * G:(kv + 1) * G].unsqueeze(2), in0=o3[:, :, 64:65],
                                                                in1=ESK[:, kv * G:(kv + 1) * G].unsqueeze(2), op=ALU.add),
                 R=[P[yb_[kv]], ESK], W=[DEN])
        S.op("dve", lambda e: e.reciprocal(out=RDEN[:, :], in_=DEN[:, :]), R=[DEN], W=[RDEN])
        for kv in range(KVH):
            o3 = pf(yb_[kv], 0, 260).rearrange("p (g d) -> p g d", d=65)
            S.op("dve", lambda e, kv=kv, o3=o3: e.tensor_tensor(
                out=YB[:, kv * 256:(kv + 1) * 256].rearrange("p (g d) -> p g d", d=HD), in0=o3[:, :, 0:HD],
                in1=RDEN[:, kv * G:(kv + 1) * G].unsqueeze(2).to_broadcast([128, G, HD]), op=ALU.mult),
                R=[P[yb_[kv]], RDEN], W=[YB])
        yab = bank()
        for h in range(NH):
            S.op("pe", lambda e, h=h: e.matmul(pf(yab, h * HD, (h + 1) * HD), lhsT=WST[:, h * 128:(h + 1) * 128],
                                               rhs=VAb[:, h * HD:(h + 1) * HD], start=True, stop=True), R=[WST, VAb], W=[P[yab]])
        for h in range(NH):
            S.op("dve", lambda e, h=h: e.scalar_tensor_tensor(out=YA[:, h * HD:(h + 1) * HD], in0=pf(yab, h * HD, (h + 1) * HD),
                                                              scalar=BSP[:, h:h + 1], in1=Ub[:, h * HD:(h + 1) * HD],
                                                              op0=ALU.add, op1=ALU.mult), R=[P[yab], BSP, Ub], W=[YA])
        S.op("act", lambda e: e.activation(out=JK[2][:, 0:1].to_broadcast([128, 512]), in_=YA[:, :], func=AF.Square, accum_out=SSO[:, 0:1]),
             R=[YA], W=[SSO, JK[2]])
        S.op("act", lambda e: e.activation(out=JK[3][:, 0:1].to_broadcast([128, 512]), in_=YB[:, :], func=AF.Square, accum_out=SSO[:, 1:2]),
             R=[YB], W=[SSO, JK[3]])
        S.op("act", lambda e: e.activation(out=VO[:, :], in_=SSO[:, :], func=AF.Ln, bias=EPSC[:, 0:1], scale=1.0 / AW),
             R=[SSO, EPSC], W=[VO])
        S.op("act", lambda e: e.activation(out=RO[:, :], in_=VO[:, :], func=AF.Exp, scale=-0.5), R=[VO], W=[RO])
        S.op("act", lambda e: e.activation(out=CAT[:, 0:512], in_=YA[:, :], func=AF.Copy, scale=RO[:, 0:1]), R=[YA, RO], W=[CAT])
        S.op("act", lambda e: e.activation(out=CAT[:, 512:1024], in_=YB[:, :], func=AF.Copy, scale=RO[:, 1:2]), R=[YB, RO], W=[CAT])
        ct = XT[1]
        transposes8(CAT, ct, ev="act")
        oa = bank2()
        for n_ in range(2):
            for k in range(8):
                S.op("pe", lambda e, k=k, n_=n_: e.matmul(
                    pf(oa + n_), lhsT=ct[:, k, :], rhs=So[n_][:, :].rearrange("p (k c) -> p k c", k=8)[:, k, :],
                    start=(k == 0), stop=(k == 7)), R=[ct, So[n_]], W=[P[oa + n_]])
        S.op("dve", lambda e: e.tensor_tensor(out=X[b][:, :], in0=p2(oa), in1=X[b][:, :], op=ALU.add),
             R=[P[oa], P[oa + 1], X[b]], W=[X[b]])
        S.op("act", lambda e: e.activation(out=JK[4][:, 0:1].to_broadcast([128, 1024]), in_=X[b][:, :], func=AF.Square, accum_out=SS2[:, b:b + 1]),
             R=[X[b]], W=[SS2, JK[4]])

    def ffn_block(l, c, b, last, fbi):
        Su = [slot_of(("up", l, c, j)) for j in range(2)]
        Sd = [slot_of(("dn", l, c, j)) for j in range(2)]
        xt = XTALL[fbi]
        if c == 0:
            ti = bank2()
            for k in range(8):
                S.op("pe", lambda e, k=k: e.transpose(out=pf(ti + k // 4, (k % 4) * 128, (k % 4 + 1) * 128),
                                                     in_=X[b][:, k * 128:(k + 1) * 128], identity=IDF[:, :]),
                     R=[X[b], IDF], W=[P[ti + k // 4]])
            S.op("act", lambda e: e.activation(out=xt[:, :, :].rearrange("p k t -> p (k t)"), in_=p2(ti), func=AF.Copy),
                 R=[P[ti], P[ti + 1]], W=[xt])
        yield
        hb = [bank(), bank()]
        for m in range(8):
            j, mm = m // 4, m % 4
            for k in range(8):
                S.op("pe", lambda e, m=m, j=j, mm=mm, k=k: e.matmul(
                    pf(hb[j], mm * 128, (mm + 1) * 128),
                    lhsT=Su[j][:, :].rearrange("p (k c) -> p k c", k=8)[:, k, mm * 128:(mm + 1) * 128],
                    rhs=xt[:, k, :], start=(k == 0), stop=(k == 7)), R=[Su[j], xt], W=[P[hb[j]]])
        RF = [U, GV]
        HT = HTL[ffnn[0] % 2]
        ffnn[0] += 1
        for j in range(2):
            S.op("act", lambda e, j=j: e.activation(out=RF[j][:, :], in_=pf(hb[j]), func=AF.Relu), R=[P[hb[j]]], W=[RF[j]])
            S.op("dve", lambda e, j=j: e.tensor_tensor(out=HT[j][:, :], in0=pf(hb[j]), in1=RF[j][:, :], op=ALU.mult),
                 R=[P[hb[j]], RF[j]], W=[HT[j]])
        yield
        d_ = bank2()
        for n_ in range(2):
            for m in range(8):
                j, mm = m // 4, m % 4
                S.op("pe", lambda e, n_=n_, j=j, mm=mm, m=m: e.matmul(
                    pf(d_ + n_), lhsT=HT[j][:, mm * 128:(mm + 1) * 128],
                    rhs=Sd[j][:, :].rearrange("p (m c) -> p m c", m=4)[:, mm, n_ * 512:(n_ + 1) * 512],
                    start=(m == 0), stop=(m == 7)), R=[HT[j], Sd[j]], W=[P[d_ + n_]])
        S.op("dve", lambda e: e.scalar_tensor_tensor(out=X[b][:, :], in0=p2(d_), scalar=R2Q[:, b:b + 1], in1=X[b][:, :],
                                                     op0=ALU.mult, op1=ALU.add), R=[P[d_], P[d_ + 1], R2Q, X[b]], W=[X[b]])
        if last and b >= H:
            S.dma("sp", y[b - H, :, :], X[b][:, :], R=[X[b]], final=True)

    for l in range(L):
        blks = order(l)
        if l > 0:
            S.barrier(MIXBUFS + list(XTALL.values()) + [ARENA])
        for i in range(3):
            S.op("pool", lambda e, i=i: e.memset(V65[i][:, :, :], 1.0), W=[V65[i]])
        S.op("pool", lambda e: e.memset(CV65[:, :, :, :], 1.0), W=[CV65])
        S.dma("act", CS[:, :], cs[:, :], W=[CS])
        S.dma("act", TRI[:, :], tril[:, :], W=[TRI])
        S.dma("act", GSV[:, :], gsv[l:l + 1, :].to_broadcast([128, AW]), W=[GSV])
        S.dma("act", GQKB[:, :], gqk[l:l + 1, :].to_broadcast([128, 128]), W=[GQKB])
        S.dma("act", BSP[:, :], bsp[l, :, :], W=[BSP])
        S.dma("act", ESK[:, :], snk[l:l + 1, :].to_broadcast([128, NH]), W=[ESK])
        S.dma("pool", MSK[:, :], masks[:, :], W=[MSK])
        S.op("act", lambda e: e.activation(out=ESK[:, :], in_=ESK[:, :], func=AF.Exp), R=[ESK], W=[ESK])
        load_spatial(l, wsT, 0)
        pump()
        def prepass(bs):
            c0, c1 = bs[0], bs[-1] + 1
            for b in bs:
                S.op("act", lambda e, b=b: e.activation(out=JK[0][:, 0:1].to_broadcast([128, 1024]), in_=X[b][:, :], func=AF.Square,
                                                        accum_out=SSX[:, b:b + 1]), R=[X[b]], W=[SSX, JK[0]])
            S.op("dve", lambda e: e.tensor_scalar(out=VX[:, c0:c1], in0=SSX[:, c0:c1], scalar1=1.0 / D, scalar2=EPS,
                                                  op0=ALU.mult, op1=ALU.add), R=[SSX], W=[VX])
            S.op("act", lambda e: e.activation(out=RSX[:, c0:c1], in_=VX[:, c0:c1], func=AF.Ln), R=[VX], W=[RSX])
            S.op("act", lambda e: e.activation(out=RSX[:, c0:c1], in_=RSX[:, c0:c1], func=AF.Exp, scale=-0.5), R=[RSX], W=[RSX])
            S.op("dve", lambda e: e.tensor_scalar(out=EPQ[:, c0:c1], in0=VX[:, c0:c1], scalar1=EPS, scalar2=None, op0=ALU.mult),
                 R=[VX], W=[EPQ])

        nfirst = 3 if len(blks) > 3 else len(blks)
        prepass(blks[:nfirst])
        chain["kt"], chain["v"] = None, None
        pend = None
        for bi, b in enumerate(blks):
            is_sample = (b == SB)
            if bi == 2 and nfirst < len(blks):
                prepass(blks[nfirst:])
            if bi == min(2, len(blks) - 1):
                cache_prep(l)
            midx = 2 if b == H else 0
            g_ = mixer_block(l, b, first=(bi == 0), is_sample=is_sample, mask_prev_idx=midx)
            ra = S.record(g_)
            rb = S.record(pend)
            S.emit_merged(ra, rb)
            pend = g_
        S.emit_merged([], S.record(pend))
        st["released"] = 22 * l + 6
        pump()
        S.barrier(MIXBUFS + list(XTALL.values()) + [ARENA])
        cur_pool[0] = "all"
        if l == 0:
            S.dma("act", ks[:, :, 0:WIN - DEC, :], ck[:, :, DEC:WIN, :])
            S.dma("act", vs[:, :, 0:WIN - DEC, :], cv[:, :, DEC:WIN, :])
        fb = [b for b in blks[1:]] if l < L - 1 else [b for b in blks if b >= H]
        S.op("dve", lambda e: e.tensor_scalar(out=VX[:, :], in0=SS2[:, :], scalar1=1.0 / D, scalar2=EPS, op0=ALU.mult, op1=ALU.add),
             R=[SS2], W=[VX])
        S.op("dve", lambda e: e.reciprocal(out=R2Q[:, :], in_=VX[:, :]), R=[VX], W=[R2Q])
        for c in range(4):
            gens = [ffn_block(l, c, b, last=(l == L - 1 and c == 3), fbi=fb.index(b)) for b in fb]
            n_ = len(gens)
            for i_ in range(n_ + 2):
                if i_ < n_:
                    next(gens[i_], None)
                if 0 <= i_ - 1 < n_:
                    next(gens[i_ - 1], None)
                if 0 <= i_ - 2 < n_:
                    next(gens[i_ - 2], None)
            st["released"] = 22 * l + 6 + 4 * (c + 1)
            if c < 3:
                pump()
    S.finish()
    return nc, es


def _consts(NB_pos):
    half = 8
    inv = np.power(np.float32(THETA), -2.0 * np.arange(half, dtype=np.float32) / 16.0).astype(np.float32)
    ang = NB_pos.astype(np.float32)[:, :, None] * inv[None, None, :]
    cs = np.concatenate([np.cos(ang), np.sin(ang)], axis=-1).astype(np.float32)
    return np.ascontiguousarray(cs.transpose(1, 0, 2).reshape(128, -1))


def _masks(first_core):
    j = np.arange(128)[:, None]
    i = np.arange(128)[None, :]
    prev = np.where(j >= i, 0.0, NEG)
    cur = np.where(j <= i, 0.0, NEG)
    first = np.full((128, 128), NEG) if first_core else prev
    scur = np.where((j // DEC == i // DEC) & (j % DEC <= i % DEC), 0.0, NEG)
    scache = np.where(j >= (i % DEC), 0.0, NEG)
    return np.ascontiguousarray(np.concatenate([prev, cur, first, scur, scache], axis=1).astype(np.float32))


def _tril2():
    j = np.arange(128)[:, None]
    i = np.arange(128)[None, :]
    a = (j <= i).astype(np.float32)
    b = ((j // DEC == i // DEC) & (j % DEC <= i % DEC)).astype(np.float32)
    return np.ascontiguousarray(np.concatenate([a, b], axis=1))


def _ws_sample(w_spatial, L):
    out = np.zeros((L, 128, NH, 128), np.float32)
    corner = w_spatial[:, :, :DEC, :DEC].transpose(0, 3, 1, 2)
    for q in range(NSEQ):
        out[:, q * DEC:(q + 1) * DEC, :, q * DEC:(q + 1) * DEC] = corner
    return np.ascontiguousarray(out.reshape(L, 128, NH * 128))


def make_in_maps(inp, L, NOWN, ncores, seg_per_batch):
    H = L
    f = lambda a: np.ascontiguousarray(np.asarray(a, dtype=np.float32))
    xp, xs = f(inp["x_prompt"]), f(inp["x_sample"])
    gT = np.concatenate([f(inp["g_mix"]).reshape(L, 8, 128).transpose(0, 2, 1),
                         np.concatenate([f(inp["g_out_a"]), f(inp["g_out_b"])], axis=1).reshape(L, 8, 128).transpose(0, 2, 1),
                         f(inp["g_ffn"]).reshape(L, 8, 128).transpose(0, 2, 1)], axis=2)
    shared = {
        "w_in": f(inp["w_in"]), "w_out": f(inp["w_out"]), "w_up": f(inp["w_up"]), "w_down": f(inp["w_down"]),
        "wsT": np.ascontiguousarray(f(inp["w_spatial"]).transpose(0, 3, 1, 2).reshape(L, 128, NH * 128)),
        "gT": np.ascontiguousarray(gT), "gsv": f(inp["g_sv"]),
        "gqk": np.ascontiguousarray(np.concatenate([f(inp["g_q"]), f(inp["g_k"])], axis=1)),
        "bsp": np.ascontiguousarray(f(inp["b_spatial"]).transpose(0, 2, 1)), "snk": f(inp["sinks"]),
        "tril": _tril2(), "identf": np.eye(128, dtype=np.float32),
        "wsTS": _ws_sample(f(inp["w_spatial"]), L),
        "bspS": np.ascontiguousarray(np.tile(f(inp["b_spatial"])[:, :, :DEC].transpose(0, 2, 1), (1, NSEQ, 1))),
    }
    maps = []
    for c in range(ncores):
        bidx, seg = c // seg_per_batch, c % seg_per_batch
        t0 = seg * NOWN * 128
        own = xp[bidx, t0:t0 + NOWN * 128].reshape(NOWN, 128, D)
        if seg > 0:
            halo = xp[bidx, t0 - H * 128:t0].reshape(H, 128, D)
        else:
            halo = np.zeros((H, 128, D), np.float32)
        samp = xs[c * NSEQ:(c + 1) * NSEQ].reshape(1, 128, D)
        pos = np.zeros((H + NOWN + 1, 128), np.float32)
        for j in range(H + NOWN):
            pos[j] = np.maximum(t0 - H * 128 + j * 128 + np.arange(128), 0)
        pos[H + NOWN] = PAST + (np.arange(128) % DEC)
        m = dict(shared)
        m["xin"] = np.ascontiguousarray(np.concatenate([halo, own, samp], axis=0))
        m["ck"] = np.ascontiguousarray(f(inp["cache_win_k"])[:, c * NSEQ:(c + 1) * NSEQ].reshape(L, NSEQ, WIN, 128))
        m["cv"] = np.ascontiguousarray(f(inp["cache_win_v"])[:, c * NSEQ:(c + 1) * NSEQ].reshape(L, NSEQ, WIN, 128))
        m["cs"] = _consts(pos)
        m["masks"] = _masks(seg == 0)
        maps.append(m)
    return maps


def assemble(res, L, NOWN, ncores, seg_per_batch, nbatch):
    SEQ = seg_per_batch * NOWN * 128
    yp = np.zeros((nbatch, SEQ, D), np.float32)
    ysm = np.zeros((ncores * NSEQ, DEC, D), np.float32)
    kpo = np.zeros((L, nbatch, WIN, KVH, HD), np.float32)
    vpo = np.zeros((L, nbatch, WIN, KVH, HD), np.float32)
    kso = np.zeros((L, ncores * NSEQ, WIN, KVH, HD), np.float32)
    vso = np.zeros((L, ncores * NSEQ, WIN, KVH, HD), np.float32)
    cvo = np.zeros((L, ncores * NSEQ, DEC, NH, HD), np.float32)
    for c in range(ncores):
        r = res[c]
        bidx, seg = c // seg_per_batch, c % seg_per_batch
        t0 = seg * NOWN * 128
        yp[bidx, t0:t0 + NOWN * 128] = r["y"][:NOWN].reshape(NOWN * 128, D)
        ysm[c * NSEQ:(c + 1) * NSEQ] = r["y"][NOWN].reshape(NSEQ, DEC, D)
        if seg == seg_per_batch - 1:
            kpo[:, bidx] = r["kp"].reshape(L, WIN, KVH, HD)
            vpo[:, bidx] = r["vp"].reshape(L, WIN, KVH, HD)
        kso[:, c * NSEQ:(c + 1) * NSEQ] = r["ks"].reshape(L, NSEQ, WIN, KVH, HD)
        vso[:, c * NSEQ:(c + 1) * NSEQ] = r["vs"].reshape(L, NSEQ, WIN, KVH, HD)
        cvo[:, c * NSEQ:(c + 1) * NSEQ] = r["cvs"].reshape(L, NSEQ, DEC, NH, HD)
    return yp, ysm, kpo, vpo, kso, vso, cvo


def kernel(**inputs):
    L, NOWN, NC_, SPB = 4, 16, 8, 4
    nc, es = build(L, NOWN)
    maps = make_in_maps(inputs, L, NOWN, NC_, SPB)
    res = run_bass_kernel_spmd(nc, maps, core_ids=list(range(NC_)))
    es.close()
    return assemble(res.results, L, NOWN, NC_, SPB, 2)
```

```python
import types
import numpy as np
from contextlib import ExitStack
import concourse.bass as bass
import concourse.mybir as mybir
from concourse.bass_utils import run_bass_kernel_spmd

F32 = mybir.dt.float32
BF16 = mybir.dt.bfloat16
AF = mybir.ActivationFunctionType
ALU = mybir.AluOpType
AX = mybir.AxisListType

D = 1024
HD = 64
NH = 8
KVH = 2
G = 4
AW = 512
DFF = 4096
INC = 1792
EPS = 1e-6
NEG = -240000.0
WIN = 128
DEC = 8
NSEQ = 16
PAST = 8192
THETA = 500000.0
NSLOT = 8


def _freeze(fn):
    if fn.__closure__ is None:
        return fn
    cells = tuple(types.CellType(c.cell_contents) for c in fn.__closure__)
    return types.FunctionType(fn.__code__, fn.__globals__, fn.__name__, fn.__defaults__, cells)


class Buf:
    def __init__(self, t, name):
        self.t = t
        self.name = name
        self.w = None
        self.r = {}
        self.dsem = None
        self.dcnt = 0

    def __getitem__(self, k):
        return self.t[k]


class Sched:
    def __init__(self, nc, es):
        self.nc = nc
        self.es = es
        self.eng = {"pe": nc.tensor, "act": nc.scalar, "dve": nc.vector, "pool": nc.gpsimd, "sp": nc.sync}
        self.sems = {}
        self.cnt = {}
        self.known = {e: {} for e in self.eng}
        for e in self.eng:
            self.sems[e] = es.enter_context(nc.semaphore("sem_" + e))
            self.cnt[e] = 0
        self.final = {}
        self.nwaits = 0
        self.rec = None

    def buf(self, shape, dtype, name):
        t = self.es.enter_context(self.nc.sbuf_tensor(name, list(shape), dtype))
        return Buf(t, name)

    def _deps(self, e, R, W, dmabuf=None):
        need = {}
        for b in R:
            if b.w is not None:
                k, v = b.w
                need[k] = max(need.get(k, 0), v)
        for b in W:
            if b.w is not None:
                k, v = b.w
                if not (dmabuf is b and k == b.dsem):
                    need[k] = max(need.get(k, 0), v)
            for k, v in b.r.items():
                need[k] = max(need.get(k, 0), v)
        kn = self.known[e]
        for k, v in need.items():
            if k == "pe" and e == "pe":
                continue
            if kn.get(k, 0) >= v:
                continue
            self.eng[e].wait_ge(self.sems[k], v)
            self.nwaits += 1
            kn[k] = v

    def _mark(self, tok, R, W):
        k, v = tok
        for b in R:
            b.r[k] = max(b.r.get(k, 0), v)
        for b in W:
            b.w = tok
            b.r = {}

    def op(self, e, fn, R=(), W=()):
        if self.rec is not None:
            self.rec.append(("op", e, _freeze(fn), tuple(R), tuple(W)))
            return
        self._deps(e, R, W)
        ins = fn(self.eng[e])
        self.cnt[e] += 1
        ins.then_inc(self.sems[e], 1)
        self._mark((e, self.cnt[e]), R, W)

    def replay(self, it):
        if it[0] == "op":
            self.op(it[1], it[2], it[3], it[4])
        else:
            self.dma(it[1], it[2], it[3], it[4], it[5], it[6], **it[7])

    def record(self, gen):
        self.rec = []
        if gen is not None:
            next(gen, None)
        r = self.rec
        self.rec = None
        return r

    def emit_merged(self, a, b):
        import os
        if os.environ.get("NOZIP"):
            for it in list(a) + list(b):
                self.replay(it)
            return
        na, nb = len(a), len(b)
        i = j = 0
        while i < na or j < nb:
            if j >= nb or (i < na and i * nb <= j * na):
                self.replay(a[i])
                i += 1
            else:
                self.replay(b[j])
                j += 1

    def dma(self, e, out, in_, R=(), W=(), final=False, **kw):
        if self.rec is not None:
            self.rec.append(("dma", e, out, in_, tuple(R), tuple(W), final, kw))
            return
        sb = None
        for b in list(W) + list(R):
            sb = b
            break
        if sb is None:
            key = "dram"
            if key not in self.sems:
                self.sems[key] = self.es.enter_context(self.nc.semaphore("sem_dram"))
                self.cnt[key] = 0
            self._deps(e, R, W)
            self.cnt[key] += 16
            self.eng[e].dma_start(out=out, in_=in_, **kw).then_inc(self.sems[key], 16)
            self.final[key] = self.cnt[key]
            return
        if sb.dsem is None:
            sb.dsem = "d_" + sb.name
            self.sems[sb.dsem] = self.es.enter_context(self.nc.semaphore("sem_" + sb.dsem))
        self._deps(e, R, W, dmabuf=sb)
        sb.dcnt += 16
        self.eng[e].dma_start(out=out, in_=in_, **kw).then_inc(self.sems[sb.dsem], 16)
        tok = (sb.dsem, sb.dcnt)
        self._mark(tok, R, W)
        if final:
            self.final[sb.dsem] = sb.dcnt

    def barrier(self, bufs):
        toks = {e: self.cnt[e] for e in ("pe", "act", "dve", "pool")}
        for b in bufs:
            if b.w is not None:
                toks[b.w[0]] = max(toks.get(b.w[0], 0), b.w[1])
            for k, v in b.r.items():
                toks[k] = max(toks.get(k, 0), v)
        for e in ("pe", "act", "dve", "pool", "sp"):
            kn = self.known[e]
            for k, v in toks.items():
                if v > 0 and kn.get(k, 0) < v:
                    self.eng[e].wait_ge(self.sems[k], v)
                    kn[k] = v
        for b in bufs:
            b.w = None
            b.r = {}

    def finish(self):
        for k, v in self.final.items():
            self.nc.sync.wait_ge(self.sems[k], v)


def build(L=4, NOWN=16, dbg=None):
    H = L
    NB = H + NOWN + 1
    SB = NB - 1
    nc = bass.Bass("TRN2", target_bir_lowering=False)
    dt = nc.dram_tensor
    xin = dt("xin", [NB, 128, D], F32, kind="ExternalInput").ap()
    ck = dt("ck", [L, NSEQ, WIN, 128], F32, kind="ExternalInput").ap()
    cv = dt("cv", [L, NSEQ, WIN, 128], F32, kind="ExternalInput").ap()
    w_in = dt("w_in", [L, D, INC], F32, kind="ExternalInput").ap()
    w_out = dt("w_out", [L, D, D], F32, kind="ExternalInput").ap()
    w_up = dt("w_up", [L, D, DFF], F32, kind="ExternalInput").ap()
    w_down = dt("w_down", [L, DFF, D], F32, kind="ExternalInput").ap()
    wsT = dt("wsT", [L, 128, NH * 128], F32, kind="ExternalInput").ap()
    gT = dt("gT", [L, 128, 24], F32, kind="ExternalInput").ap()
    gsv = dt("gsv", [L, AW], F32, kind="ExternalInput").ap()
    gqk = dt("gqk", [L, 128], F32, kind="ExternalInput").ap()
    bsp = dt("bsp", [L, 128, NH], F32, kind="ExternalInput").ap()
    snk = dt("snk", [L, NH], F32, kind="ExternalInput").ap()
    cs = dt("cs", [128, NB * 16], F32, kind="ExternalInput").ap()
    masks = dt("masks", [128, 5 * 128], F32, kind="ExternalInput").ap()
    tril = dt("tril", [128, 256], F32, kind="ExternalInput").ap()
    wsTS = dt("wsTS", [L, 128, NH * 128], F32, kind="ExternalInput").ap()
    bspS = dt("bspS", [L, 128, NH], F32, kind="ExternalInput").ap()
    identf = dt("identf", [128, 128], F32, kind="ExternalInput").ap()
    y = dt("y", [NOWN + 1, 128, D], F32, kind="ExternalOutput").ap()
    kp = dt("kp", [L, 128, 128], F32, kind="ExternalOutput").ap()
    vp = dt("vp", [L, 128, 128], F32, kind="ExternalOutput").ap()
    ks = dt("ks", [L, NSEQ, WIN, 128], F32, kind="ExternalOutput").ap()
    vs = dt("vs", [L, NSEQ, WIN, 128], F32, kind="ExternalOutput").ap()
    cvs = dt("cvs", [L, 128, AW], F32, kind="ExternalOutput").ap()

    es = ExitStack()
    S = Sched(nc, es)
    B = S.buf
    X = [B([128, D], F32, f"x{i}") for i in range(NB)]
    SLOT = [B([128, 4096], BF16, f"slot{i}") for i in range(NSLOT)]
    NSTG = 2
    STG = [B([128, 1024], F32, f"stg{i}") for i in range(NSTG)]
    NBF = NB - 1
    ARENA_BYTES = max(NBF * 2048, 40960)
    ARENA = B([128, ARENA_BYTES // 2], BF16, "arena")
    aoff = [0]

    def carve(shape, dtype, name):
        nel = int(np.prod(shape[1:]))
        esz = 4 if dtype == F32 else 2
        nbytes = (nel * esz + 31) // 32 * 32
        o = aoff[0]
        aoff[0] += nbytes
        assert aoff[0] <= ARENA_BYTES, (name, aoff[0])
        ap = ARENA.t[:, o // 2:(o + nel * esz) // 2]
        if dtype == F32:
            ap = ap.bitcast(F32)
        if len(shape) == 3:
            ap = ap.rearrange("p (a b) -> p a b", a=shape[1])
        elif len(shape) == 4:
            ap = ap.rearrange("p (a b c) -> p a b c", a=shape[1], b=shape[2])
        return Buf(ap, name)

    Cv = carve
    STATS = B([128, 224], F32, "stats")
    soff = [0]

    def small(n, name):
        o = soff[0]
        soff[0] += n
        assert soff[0] <= 224, name
        return Buf(STATS.t[:, o:o + n], name)

    JUNKT = B([128, 8], BF16, "junk")
    JK = [Buf(JUNKT.t[:, i:i + 1], f"junk{i}") for i in range(5)]
    XB = Cv([128, D], BF16, "xb")
    CAT = Cv([128, D], BF16, "cat")
    XT = [Cv([128, 8, 128], BF16, f"xt{i}") for i in range(2)]
    YA = Cv([128, 512], F32, "ya")
    YB = Cv([128, 512], F32, "yb")
    QK = Cv([128, 640], F32, "qk")
    QKB = Cv([128, 640], BF16, "qkb")
    QT = Cv([128, 512], BF16, "qt")
    KT = [Cv([128, 128], BF16, f"kt{i}") for i in range(3)]
    V65 = [Cv([128, KVH, 65], BF16, f"v65_{i}") for i in range(3)]
    PT23 = [Cv([128, 512], BF16, f"pt{i}") for i in (2, 3)]
    OT = Cv([128, 1024], F32, "ot")
    VF = Cv([128, 128], F32, "vf")
    VA = Cv([128, 512], BF16, "va")
    CKB = XB
    CV65 = Cv([128, NSEQ, KVH, 65], BF16, "cv65")
    GSV = Cv([128, AW], F32, "gsvb")
    WST = Cv([128, NH * 128], BF16, "wst")
    GQKB = Cv([128, 128], F32, "gqkb")
    QT2 = Cv([128, 512], BF16, "qt2")
    VA2 = Cv([128, 512], BF16, "va2")
    MSK = Cv([128, 5 * 128], BF16, "mskb")
    CS = Cv([128, NB * 16], F32, "csb")
    TRI = Cv([128, 256], F32, "trib")
    MIXBUFS = [XB, CAT] + XT + [YA, YB, QK, QKB, QT] + KT + V65 + PT23 + [OT, VF, VA, CV65, GSV, WST, GQKB, CS, TRI, QT2, VA2, MSK]
    XTALL = {}
    for i_ in range(NBF):
        XTALL[i_] = Buf(ARENA.t[:, i_ * 1024:(i_ + 1) * 1024].rearrange("p (k t) -> p k t", k=8), f"xtall{i_}")
    U = B([128, 512], F32, "u")
    GV = B([128, 512], F32, "gv")
    PT = [B([128, 512], BF16, f"pt{i}") for i in range(2)] + PT23
    U2 = B([128, 512], F32, "u2")
    HT2 = [B([128, 512], BF16, f"ht2_{i}") for i in range(2)]
    UL, VAL, QTL = [U, U2], [VA, VA2], [QT, QT2]
    HTL = [[PT[0], PT[1]], HT2]
    mixn = [0]
    ffnn = [0]
    BSP = B([128, NH], F32, "bspb")
    ESK = B([128, NH], F32, "esk")
    GT = B([128, 24], F32, "gtb")
    IDB = B([128, 128], BF16, "idb")
    IDF = B([128, 128], F32, "idf")
    EPSC = small(1, "epsc")
    SSX = small(NB, "ssx")
    VX = small(NB, "vx")
    RSX = small(NB, "rsx")
    EPQ = small(NB, "epq")
    SS2 = small(NB, "ss2x")
    R2Q = small(NB, "r2q")
    SSB = small(12, "ssb")
    V11 = small(12, "v11")
    R11 = small(12, "r11")
    SSO = small(2, "sso")
    VO = small(2, "vo")
    RO = small(2, "ro")
    DEN = small(NH, "den")
    RDEN = small(NH, "rden")
    PSt = es.enter_context(nc.psum_tensor("ps", [128, 8, 512], F32))
    P = [Buf(None, f"ps{i}") for i in range(8)]
    pools = {"all": [list(range(8)), 0], "A": [[0, 1, 2, 3], 0], "B": [[4, 5, 6, 7], 0]}
    cur_pool = ["all"]

    def bank():
        p = pools[cur_pool[0]]
        i = p[0][p[1] % len(p[0])]
        p[1] += 1
        return i

    def bank2():
        p = pools[cur_pool[0]]
        if p[1] % 2:
            p[1] += 1
        i = p[0][p[1] % len(p[0])]
        p[1] += 2
        return i

    def pf(i, lo=0, hi=512):
        return PSt[:, i, lo:hi]

    def pb(i):
        return PSt[:, i, :].bitcast(BF16)

    def p2(i):
        return PSt[:, i:i + 2, :].rearrange("p a b -> p (a b)")

    S.dma("sp", IDF[:, :], identf[:, :], W=[IDF])
    S.dma("pool", IDB[:, :], identf[:, :], W=[IDB])
    S.op("dve", lambda e: e.memset(EPSC[:, :], EPS), W=[EPSC])
    S.op("dve", lambda e: e.memset(SSX[:, :], 1.0), W=[SSX])
    S.op("dve", lambda e: e.memset(SS2[:, :], 1.0), W=[SS2])
    def order(l):
        return list(range(l, H)) + list(range(H, H + NOWN)) + [SB]

    for b in order(0)[:3]:
        S.dma("act", X[b][:, :], xin[b, :, :], W=[X[b]])

    pieces = []
    for l in range(L):
        for j in range(4):
            pieces.append(("in", l, j))
        for j in range(2):
            pieces.append(("out", l, j))
        for c in range(4):
            pieces += [("up", l, c, 0), ("up", l, c, 1), ("dn", l, c, 0), ("dn", l, c, 1)]
    st = {"next": 0, "released": 0, "stg": 0, "gl": -1}
    pidx = {p: i for i, p in enumerate(pieces)}

    def slot_of(p):
        return SLOT[pidx[p] % NSLOT]

    def load_piece(p):
        slot = slot_of(p)
        kind, l = p[0], p[1]
        sv = slot[:, :].rearrange("p (k c) -> p k c", k=8)
        if kind == "in":
            j = p[2]
            n = 512 if j < 3 else 256
            S.dma("pool", sv[:, :, 0:n], w_in[l, :, j * 512:j * 512 + n].rearrange("(k p) c -> p k c", p=128), W=[slot])
        elif kind == "out":
            j = p[2]
            S.dma("pool", sv[:, :, :], w_out[l, :, j * 512:(j + 1) * 512].rearrange("(k p) c -> p k c", p=128), W=[slot])
        elif kind == "up":
            c, j = p[2], p[3]
            base = c * 1024 + j * 512
            S.dma("pool", sv[:, :, :], w_up[l, :, base:base + 512].rearrange("(k p) c -> p k c", p=128), W=[slot])
        else:
            c, j = p[2], p[3]
            r0 = c * 1024 + j * 512
            S.dma("pool", slot[:, :].rearrange("p (m c) -> p m c", m=4),
                  w_down[l, r0:r0 + 512, :].rearrange("(m p) c -> p m c", p=128), W=[slot])

    def load_spatial(l, src, toff):
        for hh in range(2):
            sg = STG[st["stg"] % NSTG]
            st["stg"] += 1
            S.dma("sp", sg[:, 0:512], src[l, :, hh * 512:(hh + 1) * 512], W=[sg])
            S.op("pool", lambda e, sg=sg, hh=hh: e.tensor_tensor(
                out=WST[:, hh * 512:(hh + 1) * 512].rearrange("p (h t) -> p h t", h=4),
                in0=sg[:, 0:512].rearrange("p (h t) -> p h t", h=4),
                in1=TRI[:, toff:toff + 128].unsqueeze(1).to_broadcast([128, 4, 128]), op=ALU.mult), R=[sg, TRI], W=[WST])

    def cache_prep(l):
        cur_pool[0] = "A"
        CKT = X[0]
        ckt = CKT[:, :].bitcast(BF16).rearrange("p (s t) -> p s t", s=NSEQ)
        for hh in range(2):
            for which, src in (("k", ck), ("v", cv)):
                sg = STG[st["stg"] % NSTG]
                st["stg"] += 1
                S.dma("sp", sg[:, :].rearrange("p (s c) -> p s c", s=8),
                      src[l, hh * 8:(hh + 1) * 8, :, :].rearrange("s k c -> k s c"), W=[sg])
                if which == "k":
                    for q4 in range(2):
                        i = bank()
                        for s_ in range(4):
                            sq = q4 * 4 + s_
                            S.op("pe", lambda e, s_=s_, sq=sq, i=i, sg=sg: e.transpose(
                                out=pf(i, s_ * 128, (s_ + 1) * 128), in_=sg[:, sq * 128:(sq + 1) * 128], identity=IDF[:, :]),
                                R=[sg, IDF], W=[P[i]])
                        S.op("act", lambda e, i=i, hh=hh, q4=q4: e.activation(
                            out=ckt[:, hh * 8 + q4 * 4:hh * 8 + q4 * 4 + 4, :].rearrange("p s t -> p (s t)"), in_=pf(i), func=AF.Copy),
                            R=[P[i]], W=[CKT])
                else:
                    S.op("pool", lambda e, sg=sg, hh=hh: e.tensor_copy(
                        out=CV65[:, hh * 8:(hh + 1) * 8, :, 0:HD],
                        in_=sg[:, :].rearrange("p (s k d) -> p s k d", s=8, k=KVH)), R=[sg], W=[CV65])

    def pump(limit=None):
        while st["next"] < len(pieces) and st["next"] < st["released"] + NSLOT and (limit is None or st["next"] < limit):
            load_piece(pieces[st["next"]])
            st["next"] += 1

    pump(limit=4)
    for b in order(0)[3:]:
        S.dma("act", X[b][:, :], xin[b, :, :], R=[slot_of(("in", 0, 3))], W=[X[b]])

    chain = {"kt": None, "v": None, "n": 0}

    def transposes8(src, dstT, goff):
        i = bank()
        for k in range(8):
            S.op("pe", lambda e, k=k: e.transpose(out=pb(i)[:, k * 128:(k + 1) * 128], in_=src[:, k * 128:(k + 1) * 128],
                                                 identity=IDB[:, :]), R=[src, IDB], W=[P[i]])
        S.op("dve", lambda e: e.tensor_tensor(out=dstT[:, :, :], in0=pb(i).rearrange("p (k t) -> p k t", k=8),
                                              in1=GT[:, goff:goff + 8].unsqueeze(2).to_broadcast([128, 8, 128]), op=ALU.mult),
             R=[P[i], GT], W=[dstT])

    def mixer_block(l, b, first, is_sample, mask_prev_idx):
        kvonly = first and not is_sample
        par = mixn[0] % 2
        mixn[0] += 1
        Ub, VAb, QTb = UL[par], VAL[par], QTL[par]
        cur_pool[0] = "A"
        emit_state = is_sample or (b == H + NOWN - 1)
        rs = RSX[:, b:b + 1]
        Sl = [slot_of(("in", l, j)) for j in range(4)]
        So = [slot_of(("out", l, j)) for j in range(2)]
        S.op("act", lambda e: e.activation(out=XB[:, :], in_=X[b][:, :], func=AF.Copy), R=[X[b]], W=[XB])
        xt = XT[0]
        transposes8(XB, xt, 0)
        zb = [bank() for _ in range(4)]
        ncol = [512, 512, 512, 256]
        for k in range(8):
            for j in range(4):
                if kvonly and j < 2:
                    continue
                S.op("pe", lambda e, k=k, j=j: e.matmul(pf(zb[j], 0, ncol[j]), lhsT=xt[:, k, :],
                                                       rhs=Sl[j][:, :].rearrange("p (k c) -> p k c", k=8)[:, k, 0:ncol[j]],
                                                       start=(k == 0), stop=(k == 7)), R=[xt, Sl[j]], W=[P[zb[j]]])
        if not kvonly:
            S.op("act", lambda e: e.activation(out=Ub[:, :], in_=pf(zb[0]), func=AF.Gelu, scale=rs), R=[P[zb[0]], RSX], W=[Ub])
            S.op("act", lambda e: e.activation(out=GV[:, :], in_=pf(zb[1]), func=AF.Gelu, scale=rs), R=[P[zb[1]], RSX], W=[GV])
            S.op("act", lambda e: e.activation(out=JK[1][:, 0:1].to_broadcast([128, 512]), in_=GV[:, :], func=AF.Square, accum_out=SSB[:, 0:1]),
                 R=[GV], W=[SSB, JK[1]])
        else:
            S.op("dve", lambda e: e.memset(SSB[:, 0:1], 1.0), W=[SSB])
        S.op("act", lambda e: e.activation(out=pf(zb[0]), in_=pf(zb[2]), func=AF.Square), R=[P[zb[2]]], W=[P[zb[0]]])
        S.op("act", lambda e: e.activation(out=pf(zb[3], 256, 384), in_=pf(zb[3], 0, 128), func=AF.Square), R=[P[zb[3]]], W=[P[zb[3]]])
        S.op("dve", lambda e: e.tensor_reduce(out=SSB[:, 1:9], in_=pf(zb[0]).rearrange("p (h d) -> p h d", d=HD),
                                              axis=AX.X, op=ALU.add), R=[P[zb[0]]], W=[SSB])
        S.op("dve", lambda e: e.tensor_reduce(out=SSB[:, 9:11], in_=pf(zb[3], 256, 384).rearrange("p (h d) -> p h d", d=HD),
                                              axis=AX.X, op=ALU.add), R=[P[zb[3]]], W=[SSB])
        S.op("act", lambda e: e.activation(out=V11[:, 0:1], in_=SSB[:, 0:1], func=AF.Ln, bias=EPSC[:, 0:1], scale=1.0 / AW),
             R=[SSB, EPSC], W=[V11])
        S.op("act", lambda e: e.activation(out=V11[:, 1:11], in_=SSB[:, 1:11], func=AF.Ln, bias=EPQ[:, b:b + 1], scale=1.0 / HD),
             R=[SSB, EPQ], W=[V11])
        S.op("act", lambda e: e.activation(out=R11[:, 0:11], in_=V11[:, 0:11], func=AF.Exp, scale=-0.5), R=[V11], W=[R11])
        if not kvonly:
            S.op("dve", lambda e: e.scalar_tensor_tensor(out=VAb[:, :], in0=GV[:, :], scalar=R11[:, 0:1], in1=GSV[:, :],
                                                         op0=ALU.mult, op1=ALU.mult), R=[GV, R11, GSV], W=[VAb])
            if is_sample:
                S.op("dve", lambda e: e.scalar_tensor_tensor(out=GV[:, :], in0=GV[:, :], scalar=R11[:, 0:1], in1=GSV[:, :],
                                                             op0=ALU.mult, op1=ALU.mult), R=[GV, R11, GSV], W=[GV])
                S.dma("sp", cvs[l, :, :], GV[:, :], R=[GV], final=True)
        qk3 = QK[:, :].rearrange("p (h d) -> p h d", d=HD)
        S.op("dve", lambda e: e.tensor_tensor(out=qk3[:, 0:8, :], in0=pf(zb[2]).rearrange("p (h d) -> p h d", d=HD),
                                              in1=R11[:, 1:9].unsqueeze(2).to_broadcast([128, 8, HD]), op=ALU.mult),
             R=[P[zb[2]], R11], W=[QK])
        S.op("dve", lambda e: e.tensor_tensor(out=qk3[:, 8:10, :], in0=pf(zb[3], 0, 128).rearrange("p (h d) -> p h d", d=HD),
                                              in1=R11[:, 9:11].unsqueeze(2).to_broadcast([128, 2, HD]), op=ALU.mult),
             R=[P[zb[3]], R11], W=[QK])
        S.op("dve", lambda e: e.tensor_tensor(out=qk3[:, 0:8, :], in0=qk3[:, 0:8, :],
                                               in1=GQKB[:, 0:64].unsqueeze(1).to_broadcast([128, 8, HD]), op=ALU.mult),
             R=[QK, GQKB], W=[QK])
        S.op("dve", lambda e: e.tensor_tensor(out=qk3[:, 8:10, :], in0=qk3[:, 8:10, :],
                                               in1=GQKB[:, 64:128].unsqueeze(1).to_broadcast([128, 2, HD]), op=ALU.mult),
             R=[QK, GQKB], W=[QK])
        rt = GV[:, 0:320].rearrange("p (a h d) -> p a h d", a=4, d=8)
        cosb = CS[:, b * 16:b * 16 + 8].unsqueeze(1).to_broadcast([128, 10, 8])
        sinb = CS[:, b * 16 + 8:b * 16 + 16].unsqueeze(1).to_broadcast([128, 10, 8])
        x1 = qk3[:, :, 0:8]
        x2 = qk3[:, :, 8:16]
        for a, (xa, tb) in enumerate([(x1, cosb), (x2, sinb), (x2, cosb), (x1, sinb)]):
            S.op("dve", lambda e, a=a, xa=xa, tb=tb: e.tensor_tensor(out=rt[:, a, :, :], in0=xa, in1=tb, op=ALU.mult),
                 R=[QK, CS], W=[GV])
        S.op("dve", lambda e: e.tensor_tensor(out=x1, in0=rt[:, 0, :, :], in1=rt[:, 1, :, :], op=ALU.subtract), R=[GV], W=[QK])
        S.op("dve", lambda e: e.tensor_tensor(out=x2, in0=rt[:, 2, :, :], in1=rt[:, 3, :, :], op=ALU.add), R=[GV], W=[QK])
        S.op("dve", lambda e: e.tensor_copy(out=QKB[:, 0:512].rearrange("p (g kv d) -> p kv g d", g=G, kv=KVH),
                                             in_=QK[:, 0:512].rearrange("p (kv g d) -> p kv g d", kv=KVH, g=G)),
             R=[QK], W=[QKB])
        S.op("dve", lambda e: e.tensor_copy(out=QKB[:, 512:640], in_=QK[:, 512:640]), R=[QK], W=[QKB])
        n = chain["n"]
        vcur = V65[n % 3]
        ktcur = KT[n % 3]
        chain["n"] += 1
        S.op("dve", lambda e: e.tensor_scalar(out=vcur[:, :, 0:HD], in0=pf(zb[3], 128, 256).rearrange("p (k d) -> p k d", d=HD),
                                              scalar1=rs, scalar2=None, op0=ALU.mult), R=[P[zb[3]], RSX], W=[vcur])
        if emit_state:
            S.op("dve", lambda e: e.tensor_scalar(out=VF[:, :], in0=pf(zb[3], 128, 256), scalar1=rs, scalar2=None, op0=ALU.mult),
                 R=[P[zb[3]], RSX], W=[VF])
            if is_sample:
                for s in range(NSEQ):
                    S.dma("sp", ks[l, s, WIN - DEC:WIN, :], QK[s * DEC:(s + 1) * DEC, 512:640], R=[QK], final=True)
                    S.dma("sp", vs[l, s, WIN - DEC:WIN, :], VF[s * DEC:(s + 1) * DEC, :], R=[VF], final=True)
            else:
                S.dma("sp", kp[l, :, :], QK[:, 512:640], R=[QK], final=True)
                S.dma("sp", vp[l, :, :], VF[:, :], R=[VF], final=True)
        tb_ = bank()
        for g in range(5):
            S.op("pe", lambda e, g=g: e.transpose(out=pb(tb_)[:, g * 128:(g + 1) * 128], in_=QKB[:, g * 128:(g + 1) * 128],
                                                  identity=IDB[:, :]), R=[QKB, IDB], W=[P[tb_]])
        S.op("act", lambda e: e.activation(out=QTb[:, :], in_=pb(tb_)[:, 0:512], func=AF.Copy), R=[P[tb_]], W=[QTb])
        S.op("act", lambda e: e.activation(out=ktcur[:, :], in_=pb(tb_)[:, 512:640], func=AF.Copy), R=[P[tb_]], W=[ktcur])
        ktprev, vprev = chain["kt"], chain["v"]
        if not is_sample:
            chain["kt"], chain["v"] = ktcur, vcur
        if kvonly:
            return
        yield
        cur_pool[0] = "B"
        if is_sample:
            load_spatial(l, wsTS, 128)
            S.dma("sp", BSP[:, :], bspS[l, :, :], W=[BSP])
        qt3 = QTb[:, :].rearrange("p (g t) -> p g t", g=G)

        def maskmm(sb_, midx, start, stop):
            S.op("pe", lambda e: e.matmul(pf(sb_).rearrange("p (g t) -> p g t", g=G), lhsT=IDB[:, :],
                                          rhs=MSK[:, midx * 128:(midx + 1) * 128].unsqueeze(1).to_broadcast([128, G, 128]),
                                          start=start, stop=stop), R=[IDB, MSK], W=[P[sb_]])
        parts = []
        if is_sample:
            CKT = X[0]
            ckt = CKT[:, :].bitcast(BF16).rearrange("p (s t) -> p s t", s=NSEQ)
            for kv in range(KVH):
                sb_ = bank()
                maskmm(sb_, 4, True, False)
                for s in range(NSEQ):
                    for g in range(G):
                        S.op("pe", lambda e, kv=kv, s=s, g=g: e.matmul(
                            pf(sb_, g * 128 + s * DEC, g * 128 + (s + 1) * DEC),
                            lhsT=ckt[kv * 64:(kv + 1) * 64, s, :], rhs=qt3[kv * 64:(kv + 1) * 64, g, s * DEC:(s + 1) * DEC],
                            start=False, stop=(s == NSEQ - 1 and g == G - 1)), R=[CKT, QTb], W=[P[sb_]])
                S.op("act", lambda e, kv=kv: e.activation(out=PT[kv][:, :], in_=pf(sb_), func=AF.Exp, scale=0.125),
                     R=[P[sb_]], W=[PT[kv]])
            srcs = [(ktcur, 3)]
        else:
            srcs = [(ktprev, mask_prev_idx), (ktcur, 1)]
        for pi, (ktb, midx) in enumerate(srcs):
            for kv in range(KVH):
                sb_ = bank()
                S.op("pe", lambda e, kv=kv, ktb=ktb: e.matmul(pf(sb_), lhsT=ktb[kv * 64:(kv + 1) * 64, :],
                                                              rhs=QTb[kv * 64:(kv + 1) * 64, :], start=True, stop=False),
                     R=[ktb, QTb], W=[P[sb_]])
                maskmm(sb_, midx, False, True)
                pt = PT[2 + kv] if (is_sample or pi == 1) else PT[kv]
                S.op("act", lambda e, pt=pt: e.activation(out=pt[:, :], in_=pf(sb_), func=AF.Exp, scale=0.125),
                     R=[P[sb_]], W=[pt])
        if not is_sample:
            yb_ = [bank(), bank()]
            for kv in range(KVH):
                for g in range(G):
                    for pi, (ptb, vb) in enumerate(((PT[kv], vprev), (PT[2 + kv], vcur))):
                        S.op("pe", lambda e, kv=kv, g=g, pi=pi, ptb=ptb, vb=vb: e.matmul(
                            pf(yb_[kv], g * 65, (g + 1) * 65), lhsT=ptb[:, g * 128:(g + 1) * 128], rhs=vb[:, kv, :],
                            start=(pi == 0), stop=(pi == 1)), R=[ptb, vb], W=[P[yb_[kv]]])
        else:
            for kv in range(KVH):
                ob = bank()
                S.op("pe", lambda e, kv=kv: e.matmul(pf(ob)[0:65, :], lhsT=vcur[:, kv, :], rhs=PT[2 + kv][:, :],
                                                     start=True, stop=False), R=[vcur, PT[2 + kv]], W=[P[ob]])
                for s in range(NSEQ):
                    for g in range(G):
                        S.op("pe", lambda e, kv=kv, s=s, g=g: e.matmul(
                            pf(ob, g * 128 + s * DEC, g * 128 + (s + 1) * DEC)[0:65, :],
                            lhsT=CV65[:, s, kv, :], rhs=PT[kv][:, g * 128 + s * DEC:g * 128 + (s + 1) * DEC],
                            start=False, stop=(s == NSEQ - 1 and g == G - 1)), R=[CV65, PT[kv]], W=[P[ob]])
                S.op("act", lambda e, kv=kv: e.activation(out=OT[0:65, kv * 512:(kv + 1) * 512], in_=pf(ob)[0:65, :], func=AF.Copy),
                     R=[P[ob]], W=[OT])
            yb_ = [bank(), bank()]
            for kv in range(KVH):
                for g in range(G):
                    S.op("pe", lambda e, kv=kv, g=g: e.transpose(out=pf(yb_[kv], g * 65, (g + 1) * 65),
                                                                in_=OT[0:65, kv * 512 + g * 128:kv * 512 + (g + 1) * 128],
                                                                identity=IDF[0:65, 0:65]), R=[OT, IDF], W=[P[yb_[kv]]])
        for kv in range(KVH):
            o3 = pf(yb_[kv], 0, 260).rearrange("p (g d) -> p g d", d=65)
            S.op("dve", lambda e, kv=kv, o3=o3: e.tensor_tensor(out=DEN[:, kv * G:(kv + 1) * G].unsqueeze(2), in0=o3[:, :, 64:65],
                                                                in1=ESK[:, kv * G:(kv + 1) * G].unsqueeze(2), op=ALU.add),
                 R=[P[yb_[kv]], ESK], W=[DEN])
        S.op("dve", lambda e: e.reciprocal(out=RDEN[:, :], in_=DEN[:, :]), R=[DEN], W=[RDEN])
        for kv in range(KVH):
            o3 = pf(yb_[kv], 0, 260).rearrange("p (g d) -> p g d", d=65)
            S.op("dve", lambda e, kv=kv, o3=o3: e.tensor_tensor(
                out=YB[:, kv * 256:(kv + 1) * 256].rearrange("p (g d) -> p g d", d=HD), in0=o3[:, :, 0:HD],
                in1=RDEN[:, kv * G:(kv + 1) * G].unsqueeze(2).to_broadcast([128, G, HD]), op=ALU.mult),
                R=[P[yb_[kv]], RDEN], W=[YB])
        yab = bank()
        for h in range(NH):
            S.op("pe", lambda e, h=h: e.matmul(pf(yab, h * HD, (h + 1) * HD), lhsT=WST[:, h * 128:(h + 1) * 128],
                                               rhs=VAb[:, h * HD:(h + 1) * HD], start=True, stop=True), R=[WST, VAb], W=[P[yab]])
        for h in range(NH):
            S.op("dve", lambda e, h=h: e.scalar_tensor_tensor(out=YA[:, h * HD:(h + 1) * HD], in0=pf(yab, h * HD, (h + 1) * HD),
                                                              scalar=BSP[:, h:h + 1], in1=Ub[:, h * HD:(h + 1) * HD],
                                                              op0=ALU.add, op1=ALU.mult), R=[P[yab], BSP, Ub], W=[YA])
        S.op("act", lambda e: e.activation(out=JK[2][:, 0:1].to_broadcast([128, 512]), in_=YA[:, :], func=AF.Square, accum_out=SSO[:, 0:1]),
             R=[YA], W=[SSO, JK[2]])
        S.op("act", lambda e: e.activation(out=JK[3][:, 0:1].to_broadcast([128, 512]), in_=YB[:, :], func=AF.Square, accum_out=SSO[:, 1:2]),
             R=[YB], W=[SSO, JK[3]])
        S.op("act", lambda e: e.activation(out=VO[:, :], in_=SSO[:, :], func=AF.Ln, bias=EPSC[:, 0:1], scale=1.0 / AW),
             R=[SSO, EPSC], W=[VO])
        S.op("act", lambda e: e.activation(out=RO[:, :], in_=VO[:, :], func=AF.Exp, scale=-0.5), R=[VO], W=[RO])
        S.op("act", lambda e: e.activation(out=CAT[:, 0:512], in_=YA[:, :], func=AF.Copy, scale=RO[:, 0:1]), R=[YA, RO], W=[CAT])
        S.op("act", lambda e: e.activation(out=CAT[:, 512:1024], in_=YB[:, :], func=AF.Copy, scale=RO[:, 1:2]), R=[YB, RO], W=[CAT])
        ct = XT[1]
        transposes8(CAT, ct, 8)
        oa = bank2()
        for n_ in range(2):
            for k in range(8):
                S.op("pe", lambda e, k=k, n_=n_: e.matmul(
                    pf(oa + n_), lhsT=ct[:, k, :], rhs=So[n_][:, :].rearrange("p (k c) -> p k c", k=8)[:, k, :],
                    start=(k == 0), stop=(k == 7)), R=[ct, So[n_]], W=[P[oa + n_]])
        S.op("dve", lambda e: e.tensor_tensor(out=X[b][:, :], in0=p2(oa), in1=X[b][:, :], op=ALU.add),
             R=[P[oa], P[oa + 1], X[b]], W=[X[b]])
        S.op("act", lambda e: e.activation(out=JK[4][:, 0:1].to_broadcast([128, 1024]), in_=X[b][:, :], func=AF.Square, accum_out=SS2[:, b:b + 1]),
             R=[X[b]], W=[SS2, JK[4]])

    def ffn_block(l, c, b, last, fbi):
        Su = [slot_of(("up", l, c, j)) for j in range(2)]
        Sd = [slot_of(("dn", l, c, j)) for j in range(2)]
        xt = XTALL[fbi]
        if c == 0:
            ti = bank2()
            for k in range(8):
                S.op("pe", lambda e, k=k: e.transpose(out=pf(ti + k // 4, (k % 4) * 128, (k % 4 + 1) * 128),
                                                     in_=X[b][:, k * 128:(k + 1) * 128], identity=IDF[:, :]),
                     R=[X[b], IDF], W=[P[ti + k // 4]])
            S.op("dve", lambda e: e.tensor_tensor(out=xt[:, :, :], in0=PSt[:, ti:ti + 2, :].rearrange("p a (k t) -> p (a k) t", t=128),
                                                  in1=GT[:, 16:24].unsqueeze(2).to_broadcast([128, 8, 128]), op=ALU.mult),
                 R=[P[ti], P[ti + 1], GT], W=[xt])
        yield
        hb = [bank(), bank()]
        for m in range(8):
            j, mm = m // 4, m % 4
            for k in range(8):
                S.op("pe", lambda e, m=m, j=j, mm=mm, k=k: e.matmul(
                    pf(hb[j], mm * 128, (mm + 1) * 128),
                    lhsT=Su[j][:, :].rearrange("p (k c) -> p k c", k=8)[:, k, mm * 128:(mm + 1) * 128],
                    rhs=xt[:, k, :], start=(k == 0), stop=(k == 7)), R=[Su[j], xt], W=[P[hb[j]]])
        RF = [U, GV]
        HT = HTL[ffnn[0] % 2]
        ffnn[0] += 1
        for j in range(2):
            S.op("act", lambda e, j=j: e.activation(out=RF[j][:, :], in_=pf(hb[j]), func=AF.Relu), R=[P[hb[j]]], W=[RF[j]])
            S.op("dve", lambda e, j=j: e.tensor_tensor(out=HT[j][:, :], in0=pf(hb[j]), in1=RF[j][:, :], op=ALU.mult),
                 R=[P[hb[j]], RF[j]], W=[HT[j]])
        yield
        d_ = bank2()
        for n_ in range(2):
            for m in range(8):
                j, mm = m // 4, m % 4
                S.op("pe", lambda e, n_=n_, j=j, mm=mm, m=m: e.matmul(
                    pf(d_ + n_), lhsT=HT[j][:, mm * 128:(mm + 1) * 128],
                    rhs=Sd[j][:, :].rearrange("p (m c) -> p m c", m=4)[:, mm, n_ * 512:(n_ + 1) * 512],
                    start=(m == 0), stop=(m == 7)), R=[HT[j], Sd[j]], W=[P[d_ + n_]])
        S.op("dve", lambda e: e.scalar_tensor_tensor(out=X[b][:, :], in0=p2(d_), scalar=R2Q[:, b:b + 1], in1=X[b][:, :],
                                                     op0=ALU.mult, op1=ALU.add), R=[P[d_], P[d_ + 1], R2Q, X[b]], W=[X[b]])
        if last and b >= H:
            S.dma("sp", y[b - H, :, :], X[b][:, :], R=[X[b]], final=True)

    for l in range(L):
        blks = order(l)
        if l > 0:
            S.barrier(MIXBUFS + list(XTALL.values()) + [ARENA])
        for i in range(3):
            S.op("pool", lambda e, i=i: e.memset(V65[i][:, :, :], 1.0), W=[V65[i]])
        S.op("pool", lambda e: e.memset(CV65[:, :, :, :], 1.0), W=[CV65])
        S.dma("act", CS[:, :], cs[:, :], W=[CS])
        S.dma("act", TRI[:, :], tril[:, :], W=[TRI])
        S.dma("act", GSV[:, :], gsv[l:l + 1, :].to_broadcast([128, AW]), W=[GSV])
        S.dma("act", GQKB[:, :], gqk[l:l + 1, :].to_broadcast([128, 128]), W=[GQKB])
        S.dma("act", BSP[:, :], bsp[l, :, :], W=[BSP])
        S.dma("act", GT[:, :], gT[l, :, :], W=[GT])
        S.dma("act", ESK[:, :], snk[l:l + 1, :].to_broadcast([128, NH]), W=[ESK])
        S.dma("pool", MSK[:, :], masks[:, :], W=[MSK])
        S.op("act", lambda e: e.activation(out=ESK[:, :], in_=ESK[:, :], func=AF.Exp), R=[ESK], W=[ESK])
        load_spatial(l, wsT, 0)
        pump()
        def prepass(bs):
            c0, c1 = bs[0], bs[-1] + 1
            for b in bs:
                S.op("act", lambda e, b=b: e.activation(out=JK[0][:, 0:1].to_broadcast([128, 1024]), in_=X[b][:, :], func=AF.Square,
                                                        accum_out=SSX[:, b:b + 1]), R=[X[b]], W=[SSX, JK[0]])
            S.op("dve", lambda e: e.tensor_scalar(out=VX[:, c0:c1], in0=SSX[:, c0:c1], scalar1=1.0 / D, scalar2=EPS,
                                                  op0=ALU.mult, op1=ALU.add), R=[SSX], W=[VX])
            S.op("act", lambda e: e.activation(out=RSX[:, c0:c1], in_=VX[:, c0:c1], func=AF.Ln), R=[VX], W=[RSX])
            S.op("act", lambda e: e.activation(out=RSX[:, c0:c1], in_=RSX[:, c0:c1], func=AF.Exp, scale=-0.5), R=[RSX], W=[RSX])
            S.op("dve", lambda e: e.tensor_scalar(out=EPQ[:, c0:c1], in0=VX[:, c0:c1], scalar1=EPS, scalar2=None, op0=ALU.mult),
                 R=[VX], W=[EPQ])

        nfirst = 3 if len(blks) > 3 else len(blks)
        prepass(blks[:nfirst])
        chain["kt"], chain["v"] = None, None
        pend = None
        for bi, b in enumerate(blks):
            is_sample = (b == SB)
            if bi == 2 and nfirst < len(blks):
                prepass(blks[nfirst:])
            if bi == min(2, len(blks) - 1):
                cache_prep(l)
            midx = 2 if b == H else 0
            g_ = mixer_block(l, b, first=(bi == 0), is_sample=is_sample, mask_prev_idx=midx)
            ra = S.record(g_)
            rb = S.record(pend)
            S.emit_merged(ra, rb)
            pend = g_
        S.emit_merged([], S.record(pend))
        st["released"] = 22 * l + 6
        pump()
        S.barrier(MIXBUFS + list(XTALL.values()) + [ARENA])
        cur_pool[0] = "all"
        if l == 0:
            S.dma("act", ks[:, :, 0:WIN - DEC, :], ck[:, :, DEC:WIN, :])
            S.dma("act", vs[:, :, 0:WIN - DEC, :], cv[:, :, DEC:WIN, :])
        fb = [b for b in blks[1:]] if l < L - 1 else [b for b in blks if b >= H]
        S.op("dve", lambda e: e.tensor_scalar(out=VX[:, :], in0=SS2[:, :], scalar1=1.0 / D, scalar2=EPS, op0=ALU.mult, op1=ALU.add),
             R=[SS2], W=[VX])
        S.op("dve", lambda e: e.reciprocal(out=R2Q[:, :], in_=VX[:, :]), R=[VX], W=[R2Q])
        for c in range(4):
            gens = [ffn_block(l, c, b, last=(l == L - 1 and c == 3), fbi=fb.index(b)) for b in fb]
            n_ = len(gens)
            for i_ in range(n_ + 2):
                if i_ < n_:
                    next(gens[i_], None)
                if 0 <= i_ - 1 < n_:
                    next(gens[i_ - 1], None)
                if 0 <= i_ - 2 < n_:
                    next(gens[i_ - 2], None)
            st["released"] = 22 * l + 6 + 4 * (c + 1)
            if c < 3:
                pump()
    S.finish()
    return nc, es


def _consts(NB_pos):
    half = 8
    inv = np.power(np.float32(THETA), -2.0 * np.arange(half, dtype=np.float32) / 16.0).astype(np.float32)
    ang = NB_pos.astype(np.float32)[:, :, None] * inv[None, None, :]
    cs = np.concatenate([np.cos(ang), np.sin(ang)], axis=-1).astype(np.float32)
    return np.ascontiguousarray(cs.transpose(1, 0, 2).reshape(128, -1))


def _masks(first_core):
    j = np.arange(128)[:, None]
    i = np.arange(128)[None, :]
    prev = np.where(j >= i, 0.0, NEG)
    cur = np.where(j <= i, 0.0, NEG)
    first = np.full((128, 128), NEG) if first_core else prev
    scur = np.where((j // DEC == i // DEC) & (j % DEC <= i % DEC), 0.0, NEG)
    scache = np.where(j >= (i % DEC), 0.0, NEG)
    return np.ascontiguousarray(np.concatenate([prev, cur, first, scur, scache], axis=1).astype(np.float32))


def _tril2():
    j = np.arange(128)[:, None]
    i = np.arange(128)[None, :]
    a = (j <= i).astype(np.float32)
    b = ((j // DEC == i // DEC) & (j % DEC <= i % DEC)).astype(np.float32)
    return np.ascontiguousarray(np.concatenate([a, b], axis=1))


def _ws_sample(w_spatial, L):
    out = np.zeros((L, 128, NH, 128), np.float32)
    corner = w_spatial[:, :, :DEC, :DEC].transpose(0, 3, 1, 2)
    for q in range(NSEQ):
        out[:, q * DEC:(q + 1) * DEC, :, q * DEC:(q + 1) * DEC] = corner
    return np.ascontiguousarray(out.reshape(L, 128, NH * 128))


def make_in_maps(inp, L, NOWN, ncores, seg_per_batch):
    H = L
    f = lambda a: np.ascontiguousarray(np.asarray(a, dtype=np.float32))
    xp, xs = f(inp["x_prompt"]), f(inp["x_sample"])
    gT = np.concatenate([f(inp["g_mix"]).reshape(L, 8, 128).transpose(0, 2, 1),
                         np.concatenate([f(inp["g_out_a"]), f(inp["g_out_b"])], axis=1).reshape(L, 8, 128).transpose(0, 2, 1),
                         f(inp["g_ffn"]).reshape(L, 8, 128).transpose(0, 2, 1)], axis=2)
    shared = {
        "w_in": f(inp["w_in"]), "w_out": f(inp["w_out"]), "w_up": f(inp["w_up"]), "w_down": f(inp["w_down"]),
        "wsT": np.ascontiguousarray(f(inp["w_spatial"]).transpose(0, 3, 1, 2).reshape(L, 128, NH * 128)),
        "gT": np.ascontiguousarray(gT), "gsv": f(inp["g_sv"]),
        "gqk": np.ascontiguousarray(np.concatenate([f(inp["g_q"]), f(inp["g_k"])], axis=1)),
        "bsp": np.ascontiguousarray(f(inp["b_spatial"]).transpose(0, 2, 1)), "snk": f(inp["sinks"]),
        "tril": _tril2(), "identf": np.eye(128, dtype=np.float32),
        "wsTS": _ws_sample(f(inp["w_spatial"]), L),
        "bspS": np.ascontiguousarray(np.tile(f(inp["b_spatial"])[:, :, :DEC].transpose(0, 2, 1), (1, NSEQ, 1))),
    }
    maps = []
    for c in range(ncores):
        bidx, seg = c // seg_per_batch, c % seg_per_batch
        t0 = seg * NOWN * 128
        own = xp[bidx, t0:t0 + NOWN * 128].reshape(NOWN, 128, D)
        if seg > 0:
            halo = xp[bidx, t0 - H * 128:t0].reshape(H, 128, D)
        else:
            halo = np.zeros((H, 128, D), np.float32)
        samp = xs[c * NSEQ:(c + 1) * NSEQ].reshape(1, 128, D)
        pos = np.zeros((H + NOWN + 1, 128), np.float32)
        for j in range(H + NOWN):
            pos[j] = np.maximum(t0 - H * 128 + j * 128 + np.arange(128), 0)
        pos[H + NOWN] = PAST + (np.arange(128) % DEC)
        m = dict(shared)
        m["xin"] = np.ascontiguousarray(np.concatenate([halo, own, samp], axis=0))
        m["ck"] = np.ascontiguousarray(f(inp["cache_win_k"])[:, c * NSEQ:(c + 1) * NSEQ].reshape(L, NSEQ, WIN, 128))
        m["cv"] = np.ascontiguousarray(f(inp["cache_win_v"])[:, c * NSEQ:(c + 1) * NSEQ].reshape(L, NSEQ, WIN, 128))
        m["cs"] = _consts(pos)
        m["masks"] = _masks(seg == 0)
        maps.append(m)
    return maps


def assemble(res, L, NOWN, ncores, seg_per_batch, nbatch):
    SEQ = seg_per_batch * NOWN * 128
    yp = np.zeros((nbatch, SEQ, D), np.float32)
    ysm = np.zeros((ncores * NSEQ, DEC, D), np.float32)
    kpo = np.zeros((L, nbatch, WIN, KVH, HD), np.float32)
    vpo = np.zeros((L, nbatch, WIN, KVH, HD), np.float32)
    kso = np.zeros((L, ncores * NSEQ, WIN, KVH, HD), np.float32)
    vso = np.zeros((L, ncores * NSEQ, WIN, KVH, HD), np.float32)
    cvo = np.zeros((L, ncores * NSEQ, DEC, NH, HD), np.float32)
    for c in range(ncores):
        r = res[c]
        bidx, seg = c // seg_per_batch, c % seg_per_batch
        t0 = seg * NOWN * 128
        yp[bidx, t0:t0 + NOWN * 128] = r["y"][:NOWN].reshape(NOWN * 128, D)
        ysm[c * NSEQ:(c + 1) * NSEQ] = r["y"][NOWN].reshape(NSEQ, DEC, D)
        if seg == seg_per_batch - 1:
            kpo[:, bidx] = r["kp"].reshape(L, WIN, KVH, HD)
            vpo[:, bidx] = r["vp"].reshape(L, WIN, KVH, HD)
        kso[:, c * NSEQ:(c + 1) * NSEQ] = r["ks"].reshape(L, NSEQ, WIN, KVH, HD)
        vso[:, c * NSEQ:(c + 1) * NSEQ] = r["vs"].reshape(L, NSEQ, WIN, KVH, HD)
        cvo[:, c * NSEQ:(c + 1) * NSEQ] = r["cvs"].reshape(L, NSEQ, DEC, NH, HD)
    return yp, ysm, kpo, vpo, kso, vso, cvo


def kernel(**inputs):
    L, NOWN, NC_, SPB = 4, 16, 8, 4
    nc, es = build(L, NOWN)
    maps = make_in_maps(inputs, L, NOWN, NC_, SPB)
    res = run_bass_kernel_spmd(nc, maps, core_ids=list(range(NC_)))
    es.close()
    return assemble(res.results, L, NOWN, NC_, SPB, 2)
```

```python
import types
import numpy as np
from contextlib import ExitStack
import concourse.bass as bass
import concourse.mybir as mybir
from concourse.bass_utils import run_bass_kernel_spmd

F32 = mybir.dt.float32
BF16 = mybir.dt.bfloat16
AF = mybir.ActivationFunctionType
ALU = mybir.AluOpType
AX = mybir.AxisListType

D = 1024
HD = 64
NH = 8
KVH = 2
G = 4
AW = 512
DFF = 4096
INC = 1792
EPS = 1e-6
NEG = -240000.0
WIN = 128
DEC = 8
NSEQ = 16
PAST = 8192
THETA = 500000.0
NSLOT = 8


def _freeze(fn):
    if fn.__closure__ is None:
        return fn
    cells = tuple(types.CellType(c.cell_contents) for c in fn.__closure__)
    return types.FunctionType(fn.__code__, fn.__globals__, fn.__name__, fn.__defaults__, cells)


class Buf:
    def __init__(self, t, name):
        self.t = t
        self.name = name
        self.w = None
        self.r = {}
        self.dsem = None
        self.dcnt = 0

    def __getitem__(self, k):
        return self.t[k]


class Sched:
    def __init__(self, nc, es):
        self.nc = nc
        self.es = es
        self.eng = {"pe": nc.tensor, "act": nc.scalar, "dve": nc.vector, "pool": nc.gpsimd, "sp": nc.sync}
        self.sems = {}
        self.cnt = {}
        self.known = {e: {} for e in self.eng}
        for e in self.eng:
            self.sems[e] = es.enter_context(nc.semaphore("sem_" + e))
            self.cnt[e] = 0
        self.final = {}
        self.nwaits = 0
        self.rec = None

    def buf(self, shape, dtype, name):
        t = self.es.enter_context(self.nc.sbuf_tensor(name, list(shape), dtype))
        return Buf(t, name)

    def _deps(self, e, R, W, dmabuf=None):
        need = {}
        for b in R:
            if b.w is not None:
                k, v = b.w
                need[k] = max(need.get(k, 0), v)
        for b in W:
            if b.w is not None:
                k, v = b.w
                if not (dmabuf is b and k == b.dsem):
                    need[k] = max(need.get(k, 0), v)
            for k, v in b.r.items():
                need[k] = max(need.get(k, 0), v)
        kn = self.known[e]
        for k, v in need.items():
            if k == "pe" and e == "pe":
                continue
            if kn.get(k, 0) >= v:
                continue
            self.eng[e].wait_ge(self.sems[k], v)
            self.nwaits += 1
            kn[k] = v

    def _mark(self, tok, R, W):
        k, v = tok
        for b in R:
            b.r[k] = max(b.r.get(k, 0), v)
        for b in W:
            b.w = tok
            b.r = {}

    def op(self, e, fn, R=(), W=()):
        if self.rec is not None:
            self.rec.append(("op", e, _freeze(fn), tuple(R), tuple(W)))
            return
        self._deps(e, R, W)
        ins = fn(self.eng[e])
        self.cnt[e] += 1
        ins.then_inc(self.sems[e], 1)
        self._mark((e, self.cnt[e]), R, W)

    def replay(self, it):
        if it[0] == "op":
            self.op(it[1], it[2], it[3], it[4])
        else:
            self.dma(it[1], it[2], it[3], it[4], it[5], it[6], **it[7])

    def record(self, gen):
        self.rec = []
        if gen is not None:
            next(gen, None)
        r = self.rec
        self.rec = None
        return r

    def emit_merged(self, a, b):
        import os
        if os.environ.get("NOZIP"):
            for it in list(a) + list(b):
                self.replay(it)
            return
        na, nb = len(a), len(b)
        i = j = 0
        while i < na or j < nb:
            if j >= nb or (i < na and i * nb <= j * na):
                self.replay(a[i])
                i += 1
            else:
                self.replay(b[j])
                j += 1

    def dma(self, e, out, in_, R=(), W=(), final=False, **kw):
        if self.rec is not None:
            self.rec.append(("dma", e, out, in_, tuple(R), tuple(W), final, kw))
            return
        sb = None
        for b in list(W) + list(R):
            sb = b
            break
        if sb is None:
            key = "dram"
            if key not in self.sems:
                self.sems[key] = self.es.enter_context(self.nc.semaphore("sem_dram"))
                self.cnt[key] = 0
            self._deps(e, R, W)
            self.cnt[key] += 16
            self.eng[e].dma_start(out=out, in_=in_, **kw).then_inc(self.sems[key], 16)
            self.final[key] = self.cnt[key]
            return
        if sb.dsem is None:
            sb.dsem = "d_" + sb.name
            self.sems[sb.dsem] = self.es.enter_context(self.nc.semaphore("sem_" + sb.dsem))
        self._deps(e, R, W, dmabuf=sb)
        sb.dcnt += 16
        self.eng[e].dma_start(out=out, in_=in_, **kw).then_inc(self.sems[sb.dsem], 16)
        tok = (sb.dsem, sb.dcnt)
        self._mark(tok, R, W)
        if final:
            self.final[sb.dsem] = sb.dcnt

    def barrier(self, bufs):
        toks = {e: self.cnt[e] for e in ("pe", "act", "dve", "pool")}
        for b in bufs:
            if b.w is not None:
                toks[b.w[0]] = max(toks.get(b.w[0], 0), b.w[1])
            for k, v in b.r.items():
                toks[k] = max(toks.get(k, 0), v)
        for e in ("pe", "act", "dve", "pool", "sp"):
            kn = self.known[e]
            for k, v in toks.items():
                if v > 0 and kn.get(k, 0) < v:
                    self.eng[e].wait_ge(self.sems[k], v)
                    kn[k] = v
        for b in bufs:
            b.w = None
            b.r = {}

    def finish(self):
        for k, v in self.final.items():
            self.nc.sync.wait_ge(self.sems[k], v)


def build(L=4, NOWN=16, dbg=None):
    H = L
    NB = H + NOWN + 1
    SB = NB - 1
    nc = bass.Bass("TRN2", target_bir_lowering=False)
    dt = nc.dram_tensor
    xin = dt("xin", [NB, 128, D], F32, kind="ExternalInput").ap()
    ck = dt("ck", [L, NSEQ, WIN, 128], F32, kind="ExternalInput").ap()
    cv = dt("cv", [L, NSEQ, WIN, 128], F32, kind="ExternalInput").ap()
    w_in = dt("w_in", [L, D, INC], F32, kind="ExternalInput").ap()
    w_out = dt("w_out", [L, D, D], F32, kind="ExternalInput").ap()
    w_up = dt("w_up", [L, D, DFF], F32, kind="ExternalInput").ap()
    w_down = dt("w_down", [L, DFF, D], F32, kind="ExternalInput").ap()
    wsT = dt("wsT", [L, 128, NH * 128], F32, kind="ExternalInput").ap()
    gT = dt("gT", [L, 128, 24], F32, kind="ExternalInput").ap()
    gsv = dt("gsv", [L, AW], F32, kind="ExternalInput").ap()
    gqk = dt("gqk", [L, 128], F32, kind="ExternalInput").ap()
    bsp = dt("bsp", [L, 128, NH], F32, kind="ExternalInput").ap()
    snk = dt("snk", [L, NH], F32, kind="ExternalInput").ap()
    cs = dt("cs", [128, NB * 16], F32, kind="ExternalInput").ap()
    masks = dt("masks", [128, 5 * 128], F32, kind="ExternalInput").ap()
    tril = dt("tril", [128, 256], F32, kind="ExternalInput").ap()
    wsTS = dt("wsTS", [L, 128, NH * 128], F32, kind="ExternalInput").ap()
    bspS = dt("bspS", [L, 128, NH], F32, kind="ExternalInput").ap()
    identf = dt("identf", [128, 128], F32, kind="ExternalInput").ap()
    y = dt("y", [NOWN + 1, 128, D], F32, kind="ExternalOutput").ap()
    kp = dt("kp", [L, 128, 128], F32, kind="ExternalOutput").ap()
    vp = dt("vp", [L, 128, 128], F32, kind="ExternalOutput").ap()
    ks = dt("ks", [L, NSEQ, WIN, 128], F32, kind="ExternalOutput").ap()
    vs = dt("vs", [L, NSEQ, WIN, 128], F32, kind="ExternalOutput").ap()
    cvs = dt("cvs", [L, 128, AW], F32, kind="ExternalOutput").ap()

    es = ExitStack()
    S = Sched(nc, es)
    B = S.buf
    X = [B([128, D], F32, f"x{i}") for i in range(NB)]
    SLOT = [B([128, 4096], BF16, f"slot{i}") for i in range(NSLOT)]
    NSTG = 2
    STG = [B([128, 1024], F32, f"stg{i}") for i in range(NSTG)]
    NBF = NB - 1
    ARENA_BYTES = max(NBF * 2048, 40960)
    ARENA = B([128, ARENA_BYTES // 2], BF16, "arena")
    aoff = [0]

    def carve(shape, dtype, name):
        nel = int(np.prod(shape[1:]))
        esz = 4 if dtype == F32 else 2
        nbytes = (nel * esz + 31) // 32 * 32
        o = aoff[0]
        aoff[0] += nbytes
        assert aoff[0] <= ARENA_BYTES, (name, aoff[0])
        ap = ARENA.t[:, o // 2:(o + nel * esz) // 2]
        if dtype == F32:
            ap = ap.bitcast(F32)
        if len(shape) == 3:
            ap = ap.rearrange("p (a b) -> p a b", a=shape[1])
        elif len(shape) == 4:
            ap = ap.rearrange("p (a b c) -> p a b c", a=shape[1], b=shape[2])
        return Buf(ap, name)

    Cv = carve
    STATS = B([128, 224], F32, "stats")
    soff = [0]

    def small(n, name):
        o = soff[0]
        soff[0] += n
        assert soff[0] <= 224, name
        return Buf(STATS.t[:, o:o + n], name)

    JUNKT = B([128, 8], BF16, "junk")
    JK = [Buf(JUNKT.t[:, i:i + 1], f"junk{i}") for i in range(5)]
    XB = Cv([128, D], BF16, "xb")
    CAT = Cv([128, D], BF16, "cat")
    XT = [Cv([128, 8, 128], BF16, f"xt{i}") for i in range(2)]
    YA = Cv([128, 512], F32, "ya")
    YB = Cv([128, 512], F32, "yb")
    QK = Cv([128, 640], F32, "qk")
    QKB = Cv([128, 640], BF16, "qkb")
    QT = Cv([128, 512], BF16, "qt")
    KT = [Cv([128, 128], BF16, f"kt{i}") for i in range(3)]
    V65 = [Cv([128, KVH, 65], BF16, f"v65_{i}") for i in range(3)]
    PT23 = [Cv([128, 512], BF16, f"pt{i}") for i in (2, 3)]
    OT = Cv([128, 1024], F32, "ot")
    VF = Cv([128, 128], F32, "vf")
    VA = Cv([128, 512], BF16, "va")
    CKB = XB
    CV65 = Cv([128, NSEQ, KVH, 65], BF16, "cv65")
    GSV = Cv([128, AW], F32, "gsvb")
    WST = Cv([128, NH * 128], BF16, "wst")
    GQKB = Cv([128, 128], F32, "gqkb")
    QT2 = Cv([128, 512], BF16, "qt2")
    VA2 = Cv([128, 512], BF16, "va2")
    MSK = Cv([128, 5 * 128], BF16, "mskb")
    CS = Cv([128, NB * 16], F32, "csb")
    TRI = Cv([128, 256], F32, "trib")
    MIXBUFS = [XB, CAT] + XT + [YA, YB, QK, QKB, QT] + KT + V65 + PT23 + [OT, VF, VA, CV65, GSV, WST, GQKB, CS, TRI, QT2, VA2, MSK]
    XTALL = {}
    for i_ in range(NBF):
        XTALL[i_] = Buf(ARENA.t[:, i_ * 1024:(i_ + 1) * 1024].rearrange("p (k t) -> p k t", k=8), f"xtall{i_}")
    U = B([128, 512], F32, "u")
    GV = B([128, 512], F32, "gv")
    PT = [B([128, 512], BF16, f"pt{i}") for i in range(2)] + PT23
    U2 = B([128, 512], F32, "u2")
    HT2 = [B([128, 512], BF16, f"ht2_{i}") for i in range(2)]
    UL, VAL, QTL = [U, U2], [VA, VA2], [QT, QT2]
    HTL = [[PT[0], PT[1]], HT2]
    mixn = [0]
    ffnn = [0]
    BSP = B([128, NH], F32, "bspb")
    ESK = B([128, NH], F32, "esk")
    GT = B([128, 24], F32, "gtb")
    IDB = B([128, 128], BF16, "idb")
    IDF = B([128, 128], F32, "idf")
    EPSC = small(1, "epsc")
    SSX = small(NB, "ssx")
    VX = small(NB, "vx")
    RSX = small(NB, "rsx")
    EPQ = small(NB, "epq")
    SS2 = small(NB, "ss2x")
    R2Q = small(NB, "r2q")
    SSB = small(12, "ssb")
    V11 = small(12, "v11")
    R11 = small(12, "r11")
    SSO = small(2, "sso")
    VO = small(2, "vo")
    RO = small(2, "ro")
    DEN = small(NH, "den")
    RDEN = small(NH, "rden")
    PSt = es.enter_context(nc.psum_tensor("ps", [128, 8, 512], F32))
    P = [Buf(None, f"ps{i}") for i in range(8)]
    pools = {"all": [list(range(8)), 0], "A": [[0, 1, 2, 3], 0], "B": [[4, 5, 6, 7], 0]}
    cur_pool = ["all"]

    def bank():
        p = pools[cur_pool[0]]
        i = p[0][p[1] % len(p[0])]
        p[1] += 1
        return i

    def bank2():
        p = pools[cur_pool[0]]
        if p[1] % 2:
            p[1] += 1
        i = p[0][p[1] % len(p[0])]
        p[1] += 2
        return i

    def pf(i, lo=0, hi=512):
        return PSt[:, i, lo:hi]

    def pb(i):
        return PSt[:, i, :].bitcast(BF16)

    def p2(i):
        return PSt[:, i:i + 2, :].rearrange("p a b -> p (a b)")

    S.dma("sp", IDF[:, :], identf[:, :], W=[IDF])
    S.dma("pool", IDB[:, :], identf[:, :], W=[IDB])
    S.op("dve", lambda e: e.memset(EPSC[:, :], EPS), W=[EPSC])
    S.op("dve", lambda e: e.memset(SSX[:, :], 1.0), W=[SSX])
    S.op("dve", lambda e: e.memset(SS2[:, :], 1.0), W=[SS2])
    def order(l):
        return list(range(l, H)) + list(range(H, H + NOWN)) + [SB]

    for b in order(0)[:3]:
        S.dma("act", X[b][:, :], xin[b, :, :], W=[X[b]])

    pieces = []
    for l in range(L):
        for j in range(4):
            pieces.append(("in", l, j))
        for j in range(2):
            pieces.append(("out", l, j))
        for c in range(4):
            pieces += [("up", l, c, 0), ("up", l, c, 1), ("dn", l, c, 0), ("dn", l, c, 1)]
    st = {"next": 0, "released": 0, "stg": 0, "gl": -1}
    pidx = {p: i for i, p in enumerate(pieces)}

    def slot_of(p):
        return SLOT[pidx[p] % NSLOT]

    def load_gains(l):
        if st["gl"] == l:
            return
        st["gl"] = l
        S.dma("sp", GT[:, :], gT[l, :, :], W=[GT])

    def staged(slot, src_fn, ncols, goff):
        sv = slot[:, :].rearrange("p (k c) -> p k c", k=8)
        kstep = 1024 // ncols
        for k0 in range(0, 8, kstep):
            sg = STG[st["stg"] % NSTG]
            st["stg"] += 1
            sgv = sg[:, :].rearrange("p (k c) -> p k c", k=kstep)
            S.dma("sp", sgv, src_fn(k0, k0 + kstep), W=[sg])
            S.op("pool", lambda e, sgv=sgv, k0=k0: e.tensor_tensor(
                out=sv[:, k0:k0 + kstep, 0:ncols], in0=sgv,
                in1=GT[:, goff + k0:goff + k0 + kstep].unsqueeze(2).to_broadcast([128, kstep, ncols]), op=ALU.mult),
                R=[sg, GT], W=[slot])

    def load_piece(p):
        slot = slot_of(p)
        kind, l = p[0], p[1]
        if kind == "in":
            load_gains(l)
            j = p[2]
            n = 512 if j < 3 else 256
            staged(slot, lambda a, b_, l=l, j=j, n=n: w_in[l, a * 128:b_ * 128, j * 512:j * 512 + n].rearrange("(k p) c -> p k c", p=128), n, 0)
        elif kind == "out":
            j = p[2]
            staged(slot, lambda a, b_, l=l, j=j: w_out[l, a * 128:b_ * 128, j * 512:(j + 1) * 512].rearrange("(k p) c -> p k c", p=128), 512, 8)
        elif kind == "up":
            c, j = p[2], p[3]
            base = c * 1024 + j * 512
            S.dma("pool", slot[:, :].rearrange("p (k c) -> p k c", k=8),
                  w_up[l, :, base:base + 512].rearrange("(k p) c -> p k c", p=128), W=[slot])
        else:
            c, j = p[2], p[3]
            r0 = c * 1024 + j * 512
            S.dma("pool", slot[:, :].rearrange("p (m c) -> p m c", m=4),
                  w_down[l, r0:r0 + 512, :].rearrange("(m p) c -> p m c", p=128), W=[slot])

    def load_spatial(l, src, toff):
        for hh in range(2):
            sg = STG[st["stg"] % NSTG]
            st["stg"] += 1
            S.dma("sp", sg[:, 0:512], src[l, :, hh * 512:(hh + 1) * 512], W=[sg])
            S.op("pool", lambda e, sg=sg, hh=hh: e.tensor_tensor(
                out=WST[:, hh * 512:(hh + 1) * 512].rearrange("p (h t) -> p h t", h=4),
                in0=sg[:, 0:512].rearrange("p (h t) -> p h t", h=4),
                in1=TRI[:, toff:toff + 128].unsqueeze(1).to_broadcast([128, 4, 128]), op=ALU.mult), R=[sg, TRI], W=[WST])

    def cache_prep(l):
        cur_pool[0] = "A"
        CKT = X[0]
        ckt = CKT[:, :].bitcast(BF16).rearrange("p (s t) -> p s t", s=NSEQ)
        for hh in range(2):
            for which, src in (("k", ck), ("v", cv)):
                sg = STG[st["stg"] % NSTG]
                st["stg"] += 1
                S.dma("sp", sg[:, :].rearrange("p (s c) -> p s c", s=8),
                      src[l, hh * 8:(hh + 1) * 8, :, :].rearrange("s k c -> k s c"), W=[sg])
                if which == "k":
                    for q4 in range(2):
                        i = bank()
                        for s_ in range(4):
                            sq = q4 * 4 + s_
                            S.op("pe", lambda e, s_=s_, sq=sq, i=i, sg=sg: e.transpose(
                                out=pf(i, s_ * 128, (s_ + 1) * 128), in_=sg[:, sq * 128:(sq + 1) * 128], identity=IDF[:, :]),
                                R=[sg, IDF], W=[P[i]])
                        S.op("act", lambda e, i=i, hh=hh, q4=q4: e.activation(
                            out=ckt[:, hh * 8 + q4 * 4:hh * 8 + q4 * 4 + 4, :].rearrange("p s t -> p (s t)"), in_=pf(i), func=AF.Copy),
                            R=[P[i]], W=[CKT])
                else:
                    S.op("pool", lambda e, sg=sg, hh=hh: e.tensor_copy(
                        out=CV65[:, hh * 8:(hh + 1) * 8, :, 0:HD],
                        in_=sg[:, :].rearrange("p (s k d) -> p s k d", s=8, k=KVH)), R=[sg], W=[CV65])

    def pump(limit=None):
        while st["next"] < len(pieces) and st["next"] < st["released"] + NSLOT and (limit is None or st["next"] < limit):
            load_piece(pieces[st["next"]])
            st["next"] += 1

    pump(limit=4)
    for b in order(0)[3:]:
        S.dma("act", X[b][:, :], xin[b, :, :], R=[slot_of(("in", 0, 3))], W=[X[b]])

    chain = {"kt": None, "v": None, "n": 0}

    def transposes8(src, dstT, ev="dve"):
        i = bank()
        for k in range(8):
            S.op("pe", lambda e, k=k: e.transpose(out=pb(i)[:, k * 128:(k + 1) * 128], in_=src[:, k * 128:(k + 1) * 128],
                                                 identity=IDB[:, :]), R=[src, IDB], W=[P[i]])
        if ev == "dve":
            S.op("dve", lambda e: e.tensor_copy(out=dstT[:, :, :].rearrange("p k t -> p (k t)"), in_=pb(i)), R=[P[i]], W=[dstT])
        else:
            S.op("act", lambda e: e.activation(out=dstT[:, :, :].rearrange("p k t -> p (k t)"), in_=pb(i), func=AF.Copy),
                 R=[P[i]], W=[dstT])

    def mixer_block(l, b, first, is_sample, mask_prev_idx):
        kvonly = first and not is_sample
        par = mixn[0] % 2
        mixn[0] += 1
        Ub, VAb, QTb = UL[par], VAL[par], QTL[par]
        cur_pool[0] = "A"
        emit_state = is_sample or (b == H + NOWN - 1)
        rs = RSX[:, b:b + 1]
        Sl = [slot_of(("in", l, j)) for j in range(4)]
        So = [slot_of(("out", l, j)) for j in range(2)]
        S.op("act", lambda e: e.activation(out=XB[:, :], in_=X[b][:, :], func=AF.Copy), R=[X[b]], W=[XB])
        xt = XT[0]
        transposes8(XB, xt)
        zb = [bank() for _ in range(4)]
        ncol = [512, 512, 512, 256]
        for k in range(8):
            for j in range(4):
                if kvonly and j < 2:
                    continue
                S.op("pe", lambda e, k=k, j=j: e.matmul(pf(zb[j], 0, ncol[j]), lhsT=xt[:, k, :],
                                                       rhs=Sl[j][:, :].rearrange("p (k c) -> p k c", k=8)[:, k, 0:ncol[j]],
                                                       start=(k == 0), stop=(k == 7)), R=[xt, Sl[j]], W=[P[zb[j]]])
        if not kvonly:
            S.op("act", lambda e: e.activation(out=Ub[:, :], in_=pf(zb[0]), func=AF.Gelu, scale=rs), R=[P[zb[0]], RSX], W=[Ub])
            S.op("act", lambda e: e.activation(out=GV[:, :], in_=pf(zb[1]), func=AF.Gelu, scale=rs), R=[P[zb[1]], RSX], W=[GV])
            S.op("act", lambda e: e.activation(out=JK[1][:, 0:1].to_broadcast([128, 512]), in_=GV[:, :], func=AF.Square, accum_out=SSB[:, 0:1]),
                 R=[GV], W=[SSB, JK[1]])
        else:
            S.op("dve", lambda e: e.memset(SSB[:, 0:1], 1.0), W=[SSB])
        S.op("act", lambda e: e.activation(out=pf(zb[0]), in_=pf(zb[2]), func=AF.Square), R=[P[zb[2]]], W=[P[zb[0]]])
        S.op("act", lambda e: e.activation(out=pf(zb[3], 256, 384), in_=pf(zb[3], 0, 128), func=AF.Square), R=[P[zb[3]]], W=[P[zb[3]]])
        S.op("dve", lambda e: e.tensor_reduce(out=SSB[:, 1:9], in_=pf(zb[0]).rearrange("p (h d) -> p h d", d=HD),
                                              axis=AX.X, op=ALU.add), R=[P[zb[0]]], W=[SSB])
        S.op("dve", lambda e: e.tensor_reduce(out=SSB[:, 9:11], in_=pf(zb[3], 256, 384).rearrange("p (h d) -> p h d", d=HD),
                                              axis=AX.X, op=ALU.add), R=[P[zb[3]]], W=[SSB])
        S.op("act", lambda e: e.activation(out=V11[:, 0:1], in_=SSB[:, 0:1], func=AF.Ln, bias=EPSC[:, 0:1], scale=1.0 / AW),
             R=[SSB, EPSC], W=[V11])
        S.op("act", lambda e: e.activation(out=V11[:, 1:11], in_=SSB[:, 1:11], func=AF.Ln, bias=EPQ[:, b:b + 1], scale=1.0 / HD),
             R=[SSB, EPQ], W=[V11])
        S.op("act", lambda e: e.activation(out=R11[:, 0:11], in_=V11[:, 0:11], func=AF.Exp, scale=-0.5), R=[V11], W=[R11])
        if not kvonly:
            S.op("dve", lambda e: e.scalar_tensor_tensor(out=VAb[:, :], in0=GV[:, :], scalar=R11[:, 0:1], in1=GSV[:, :],
                                                         op0=ALU.mult, op1=ALU.mult), R=[GV, R11, GSV], W=[VAb])
            if is_sample:
                S.op("dve", lambda e: e.scalar_tensor_tensor(out=GV[:, :], in0=GV[:, :], scalar=R11[:, 0:1], in1=GSV[:, :],
                                                             op0=ALU.mult, op1=ALU.mult), R=[GV, R11, GSV], W=[GV])
                S.dma("sp", cvs[l, :, :], GV[:, :], R=[GV], final=True)
        qk3 = QK[:, :].rearrange("p (h d) -> p h d", d=HD)
        S.op("dve", lambda e: e.tensor_tensor(out=qk3[:, 0:8, :], in0=pf(zb[2]).rearrange("p (h d) -> p h d", d=HD),
                                              in1=R11[:, 1:9].unsqueeze(2).to_broadcast([128, 8, HD]), op=ALU.mult),
             R=[P[zb[2]], R11], W=[QK])
        S.op("dve", lambda e: e.tensor_tensor(out=qk3[:, 8:10, :], in0=pf(zb[3], 0, 128).rearrange("p (h d) -> p h d", d=HD),
                                              in1=R11[:, 9:11].unsqueeze(2).to_broadcast([128, 2, HD]), op=ALU.mult),
             R=[P[zb[3]], R11], W=[QK])
        S.op("dve", lambda e: e.tensor_tensor(out=qk3[:, 0:8, :], in0=qk3[:, 0:8, :],
                                               in1=GQKB[:, 0:64].unsqueeze(1).to_broadcast([128, 8, HD]), op=ALU.mult),
             R=[QK, GQKB], W=[QK])
        S.op("dve", lambda e: e.tensor_tensor(out=qk3[:, 8:10, :], in0=qk3[:, 8:10, :],
                                               in1=GQKB[:, 64:128].unsqueeze(1).to_broadcast([128, 2, HD]), op=ALU.mult),
             R=[QK, GQKB], W=[QK])
        rt = GV[:, 0:320].rearrange("p (a h d) -> p a h d", a=4, d=8)
        cosb = CS[:, b * 16:b * 16 + 8].unsqueeze(1).to_broadcast([128, 10, 8])
        sinb = CS[:, b * 16 + 8:b * 16 + 16].unsqueeze(1).to_broadcast([128, 10, 8])
        x1 = qk3[:, :, 0:8]
        x2 = qk3[:, :, 8:16]
        for a, (xa, tb) in enumerate([(x1, cosb), (x2, sinb), (x2, cosb), (x1, sinb)]):
            S.op("dve", lambda e, a=a, xa=xa, tb=tb: e.tensor_tensor(out=rt[:, a, :, :], in0=xa, in1=tb, op=ALU.mult),
                 R=[QK, CS], W=[GV])
        S.op("dve", lambda e: e.tensor_tensor(out=x1, in0=rt[:, 0, :, :], in1=rt[:, 1, :, :], op=ALU.subtract), R=[GV], W=[QK])
        S.op("dve", lambda e: e.tensor_tensor(out=x2, in0=rt[:, 2, :, :], in1=rt[:, 3, :, :], op=ALU.add), R=[GV], W=[QK])
        S.op("dve", lambda e: e.tensor_copy(out=QKB[:, 0:512].rearrange("p (g kv d) -> p kv g d", g=G, kv=KVH),
                                             in_=QK[:, 0:512].rearrange("p (kv g d) -> p kv g d", kv=KVH, g=G)),
             R=[QK], W=[QKB])
        S.op("dve", lambda e: e.tensor_copy(out=QKB[:, 512:640], in_=QK[:, 512:640]), R=[QK], W=[QKB])
        n = chain["n"]
        vcur = V65[n % 3]
        ktcur = KT[n % 3]
        chain["n"] += 1
        S.op("dve", lambda e: e.tensor_scalar(out=vcur[:, :, 0:HD], in0=pf(zb[3], 128, 256).rearrange("p (k d) -> p k d", d=HD),
                                              scalar1=rs, scalar2=None, op0=ALU.mult), R=[P[zb[3]], RSX], W=[vcur])
        if emit_state:
            S.op("dve", lambda e: e.tensor_scalar(out=VF[:, :], in0=pf(zb[3], 128, 256), scalar1=rs, scalar2=None, op0=ALU.mult),
                 R=[P[zb[3]], RSX], W=[VF])
            if is_sample:
                for s in range(NSEQ):
                    S.dma("sp", ks[l, s, WIN - DEC:WIN, :], QK[s * DEC:(s + 1) * DEC, 512:640], R=[QK], final=True)
                    S.dma("sp", vs[l, s, WIN - DEC:WIN, :], VF[s * DEC:(s + 1) * DEC, :], R=[VF], final=True)
            else:
                S.dma("sp", kp[l, :, :], QK[:, 512:640], R=[QK], final=True)
                S.dma("sp", vp[l, :, :], VF[:, :], R=[VF], final=True)
        tb_ = bank()
        for g in range(5):
            S.op("pe", lambda e, g=g: e.transpose(out=pb(tb_)[:, g * 128:(g + 1) * 128], in_=QKB[:, g * 128:(g + 1) * 128],
                                                  identity=IDB[:, :]), R=[QKB, IDB], W=[P[tb_]])
        S.op("act", lambda e: e.activation(out=QTb[:, :], in_=pb(tb_)[:, 0:512], func=AF.Copy), R=[P[tb_]], W=[QTb])
        S.op("act", lambda e: e.activation(out=ktcur[:, :], in_=pb(tb_)[:, 512:640], func=AF.Copy), R=[P[tb_]], W=[ktcur])
        ktprev, vprev = chain["kt"], chain["v"]
        if not is_sample:
            chain["kt"], chain["v"] = ktcur, vcur
        if kvonly:
            return
        yield
        cur_pool[0] = "B"
        if is_sample:
            load_spatial(l, wsTS, 128)
            S.dma("sp", BSP[:, :], bspS[l, :, :], W=[BSP])
        qt3 = QTb[:, :].rearrange("p (g t) -> p g t", g=G)

        def maskmm(sb_, midx, start, stop):
            S.op("pe", lambda e: e.matmul(pf(sb_).rearrange("p (g t) -> p g t", g=G), lhsT=IDB[:, :],
                                          rhs=MSK[:, midx * 128:(midx + 1) * 128].unsqueeze(1).to_broadcast([128, G, 128]),
                                          start=start, stop=stop), R=[IDB, MSK], W=[P[sb_]])
        parts = []
        if is_sample:
            CKT = X[0]
            ckt = CKT[:, :].bitcast(BF16).rearrange("p (s t) -> p s t", s=NSEQ)
            for kv in range(KVH):
                sb_ = bank()
                maskmm(sb_, 4, True, False)
                for s in range(NSEQ):
                    for g in range(G):
                        S.op("pe", lambda e, kv=kv, s=s, g=g: e.matmul(
                            pf(sb_, g * 128 + s * DEC, g * 128 + (s + 1) * DEC),
                            lhsT=ckt[kv * 64:(kv + 1) * 64, s, :], rhs=qt3[kv * 64:(kv + 1) * 64, g, s * DEC:(s + 1) * DEC],
                            start=False, stop=(s == NSEQ - 1 and g == G - 1)), R=[CKT, QTb], W=[P[sb_]])
                S.op("act", lambda e, kv=kv: e.activation(out=PT[kv][:, :], in_=pf(sb_), func=AF.Exp, scale=0.125),
                     R=[P[sb_]], W=[PT[kv]])
            srcs = [(ktcur, 3)]
        else:
            srcs = [(ktprev, mask_prev_idx), (ktcur, 1)]
        for pi, (ktb, midx) in enumerate(srcs):
            for kv in range(KVH):
                sb_ = bank()
                S.op("pe", lambda e, kv=kv, ktb=ktb: e.matmul(pf(sb_), lhsT=ktb[kv * 64:(kv + 1) * 64, :],
                                                              rhs=QTb[kv * 64:(kv + 1) * 64, :], start=True, stop=False),
                     R=[ktb, QTb], W=[P[sb_]])
                maskmm(sb_, midx, False, True)
                pt = PT[2 + kv] if (is_sample or pi == 1) else PT[kv]
                S.op("act", lambda e, pt=pt: e.activation(out=pt[:, :], in_=pf(sb_), func=AF.Exp, scale=0.125),
                     R=[P[sb_]], W=[pt])
        if not is_sample:
            yb_ = [bank(), bank()]
            for kv in range(KVH):
                for g in range(G):
                    for pi, (ptb, vb) in enumerate(((PT[kv], vprev), (PT[2 + kv], vcur))):
                        S.op("pe", lambda e, kv=kv, g=g, pi=pi, ptb=ptb, vb=vb: e.matmul(
                            pf(yb_[kv], g * 65, (g + 1) * 65), lhsT=ptb[:, g * 128:(g + 1) * 128], rhs=vb[:, kv, :],
                            start=(pi == 0), stop=(pi == 1)), R=[ptb, vb], W=[P[yb_[kv]]])
        else:
            for kv in range(KVH):
                ob = bank()
                S.op("pe", lambda e, kv=kv: e.matmul(pf(ob)[0:65, :], lhsT=vcur[:, kv, :], rhs=PT[2 + kv][:, :],
                                                     start=True, stop=False), R=[vcur, PT[2 + kv]], W=[P[ob]])
                for s in range(NSEQ):
                    for g in range(G):
                        S.op("pe", lambda e, kv=kv, s=s, g=g: e.matmul(
                            pf(ob, g * 128 + s * DEC, g * 128 + (s + 1) * DEC)[0:65, :],
                            lhsT=CV65[:, s, kv, :], rhs=PT[kv][:, g * 128 + s * DEC:g * 128 + (s + 1) * DEC],
                            start=False, stop=(s == NSEQ - 1 and g == G - 1)), R=[CV65, PT[kv]], W=[P[ob]])
                S.op("act", lambda e, kv=kv: e.activation(out=OT[0:65, kv * 512:(kv + 1) * 512], in_=pf(ob)[0:65, :], func=AF.Copy),
                     R=[P[ob]], W=[OT])
            yb_ = [bank(), bank()]
            for kv in range(KVH):
                for g in range(G):
                    S.op("pe", lambda e, kv=kv, g=g: e.transpose(out=pf(yb_[kv], g * 65, (g + 1) * 65),
                                                                in_=OT[0:65, kv * 512 + g * 128:kv * 512 + (g + 1) * 128],
                                                                identity=IDF[0:65, 0:65]), R=[OT, IDF], W=[P[yb_[kv]]])
        for kv in range(KVH):
            o3 = pf(yb_[kv], 0, 260).rearrange("p (g d) -> p g d", d=65)
            S.op("dve", lambda e, kv=kv, o3=o3: e.tensor_tensor(out=DEN[:, kv * G:(kv + 1) * G].unsqueeze(2), in0=o3[:, :, 64:65],
                                                                in1=ESK[:, kv * G:(kv + 1) * G].unsqueeze(2), op=ALU.add),
                 R=[P[yb_[kv]], ESK], W=[DEN])
        S.op("dve", lambda e: e.reciprocal(out=RDEN[:, :], in_=DEN[:, :]), R=[DEN], W=[RDEN])
        for kv in range(KVH):
            o3 = pf(yb_[kv], 0, 260).rearrange("p (g d) -> p g d", d=65)
            S.op("dve", lambda e, kv=kv, o3=o3: e.tensor_tensor(
                out=YB[:, kv * 256:(kv + 1) * 256].rearrange("p (g d) -> p g d", d=HD), in0=o3[:, :, 0:HD],
                in1=RDEN[:, kv * G:(kv + 1) * G].unsqueeze(2).to_broadcast([128, G, HD]), op=ALU.mult),
                R=[P[yb_[kv]], RDEN], W=[YB])
        yab = bank()
        for h in range(NH):
            S.op("pe", lambda e, h=h: e.matmul(pf(yab, h * HD, (h + 1) * HD), lhsT=WST[:, h * 128:(h + 1) * 128],
                                               rhs=VAb[:, h * HD:(h + 1) * HD], start=True, stop=True), R=[WST, VAb], W=[P[yab]])
        for h in range(NH):
            S.op("dve", lambda e, h=h: e.scalar_tensor_tensor(out=YA[:, h * HD:(h + 1) * HD], in0=pf(yab, h * HD, (h + 1) * HD),
                                                              scalar=BSP[:, h:h + 1], in1=Ub[:, h * HD:(h + 1) * HD],
                                                              op0=ALU.add, op1=ALU.mult), R=[P[yab], BSP, Ub], W=[YA])
        S.op("act", lambda e: e.activation(out=JK[2][:, 0:1].to_broadcast([128, 512]), in_=YA[:, :], func=AF.Square, accum_out=SSO[:, 0:1]),
             R=[YA], W=[SSO, JK[2]])
        S.op("act", lambda e: e.activation(out=JK[3][:, 0:1].to_broadcast([128, 512]), in_=YB[:, :], func=AF.Square, accum_out=SSO[:, 1:2]),
             R=[YB], W=[SSO, JK[3]])
        S.op("act", lambda e: e.activation(out=VO[:, :], in_=SSO[:, :], func=AF.Ln, bias=EPSC[:, 0:1], scale=1.0 / AW),
             R=[SSO, EPSC], W=[VO])
        S.op("act", lambda e: e.activation(out=RO[:, :], in_=VO[:, :], func=AF.Exp, scale=-0.5), R=[VO], W=[RO])
        S.op("act", lambda e: e.activation(out=CAT[:, 0:512], in_=YA[:, :], func=AF.Copy, scale=RO[:, 0:1]), R=[YA, RO], W=[CAT])
        S.op("act", lambda e: e.activation(out=CAT[:, 512:1024], in_=YB[:, :], func=AF.Copy, scale=RO[:, 1:2]), R=[YB, RO], W=[CAT])
        ct = XT[1]
        transposes8(CAT, ct, ev="act")
        oa = bank2()
        for n_ in range(2):
            for k in range(8):
                S.op("pe", lambda e, k=k, n_=n_: e.matmul(
                    pf(oa + n_), lhsT=ct[:, k, :], rhs=So[n_][:, :].rearrange("p (k c) -> p k c", k=8)[:, k, :],
                    start=(k == 0), stop=(k == 7)), R=[ct, So[n_]], W=[P[oa + n_]])
        S.op("dve", lambda e: e.tensor_tensor(out=X[b][:, :], in0=p2(oa), in1=X[b][:, :], op=ALU.add),
             R=[P[oa], P[oa + 1], X[b]], W=[X[b]])
        S.op("act", lambda e: e.activation(out=JK[4][:, 0:1].to_broadcast([128, 1024]), in_=X[b][:, :], func=AF.Square, accum_out=SS2[:, b:b + 1]),
             R=[X[b]], W=[SS2, JK[4]])

    def ffn_block(l, c, b, last, fbi):
        Su = [slot_of(("up", l, c, j)) for j in range(2)]
        Sd = [slot_of(("dn", l, c, j)) for j in range(2)]
        xt = XTALL[fbi]
        if c == 0:
            ti = bank2()
            for k in range(8):
                S.op("pe", lambda e, k=k: e.transpose(out=pf(ti + k // 4, (k % 4) * 128, (k % 4 + 1) * 128),
                                                     in_=X[b][:, k * 128:(k + 1) * 128], identity=IDF[:, :]),
                     R=[X[b], IDF], W=[P[ti + k // 4]])
            S.op("dve", lambda e: e.tensor_tensor(out=xt[:, :, :], in0=PSt[:, ti:ti + 2, :].rearrange("p a (k t) -> p (a k) t", t=128),
                                                  in1=GT[:, 16:24].unsqueeze(2).to_broadcast([128, 8, 128]), op=ALU.mult),
                 R=[P[ti], P[ti + 1], GT], W=[xt])
        yield
        hb = [bank(), bank()]
        for m in range(8):
            j, mm = m // 4, m % 4
            for k in range(8):
                S.op("pe", lambda e, m=m, j=j, mm=mm, k=k: e.matmul(
                    pf(hb[j], mm * 128, (mm + 1) * 128),
                    lhsT=Su[j][:, :].rearrange("p (k c) -> p k c", k=8)[:, k, mm * 128:(mm + 1) * 128],
                    rhs=xt[:, k, :], start=(k == 0), stop=(k == 7)), R=[Su[j], xt], W=[P[hb[j]]])
        RF = [U, GV]
        HT = HTL[ffnn[0] % 2]
        ffnn[0] += 1
        for j in range(2):
            S.op("act", lambda e, j=j: e.activation(out=RF[j][:, :], in_=pf(hb[j]), func=AF.Relu), R=[P[hb[j]]], W=[RF[j]])
            S.op("dve", lambda e, j=j: e.tensor_tensor(out=HT[j][:, :], in0=pf(hb[j]), in1=RF[j][:, :], op=ALU.mult),
                 R=[P[hb[j]], RF[j]], W=[HT[j]])
        yield
        d_ = bank2()
        for n_ in range(2):
            for m in range(8):
                j, mm = m // 4, m % 4
                S.op("pe", lambda e, n_=n_, j=j, mm=mm, m=m: e.matmul(
                    pf(d_ + n_), lhsT=HT[j][:, mm * 128:(mm + 1) * 128],
                    rhs=Sd[j][:, :].rearrange("p (m c) -> p m c", m=4)[:, mm, n_ * 512:(n_ + 1) * 512],
                    start=(m == 0), stop=(m == 7)), R=[HT[j], Sd[j]], W=[P[d_ + n_]])
        S.op("dve", lambda e: e.scalar_tensor_tensor(out=X[b][:, :], in0=p2(d_), scalar=R2Q[:, b:b + 1], in1=X[b][:, :],
                                                     op0=ALU.mult, op1=ALU.add), R=[P[d_], P[d_ + 1], R2Q, X[b]], W=[X[b]])
        if last and b >= H:
            S.dma("sp", y[b - H, :, :], X[b][:, :], R=[X[b]], final=True)

    for l in range(L):
        blks = order(l)
        if l > 0:
            S.barrier(MIXBUFS + list(XTALL.values()) + [ARENA])
        for i in range(3):
            S.op("pool", lambda e, i=i: e.memset(V65[i][:, :, :], 1.0), W=[V65[i]])
        S.op("pool", lambda e: e.memset(CV65[:, :, :, :], 1.0), W=[CV65])
        S.dma("act", CS[:, :], cs[:, :], W=[CS])
        S.dma("act", TRI[:, :], tril[:, :], W=[TRI])
        S.dma("act", GSV[:, :], gsv[l:l + 1, :].to_broadcast([128, AW]), W=[GSV])
        S.dma("act", GQKB[:, :], gqk[l:l + 1, :].to_broadcast([128, 128]), W=[GQKB])
        S.dma("act", BSP[:, :], bsp[l, :, :], W=[BSP])
        S.dma("act", ESK[:, :], snk[l:l + 1, :].to_broadcast([128, NH]), W=[ESK])
        S.dma("pool", MSK[:, :], masks[:, :], W=[MSK])
        S.op("act", lambda e: e.activation(out=ESK[:, :], in_=ESK[:, :], func=AF.Exp), R=[ESK], W=[ESK])
        load_spatial(l, wsT, 0)
        pump()
        def prepass(bs):
            c0, c1 = bs[0], bs[-1] + 1
            for b in bs:
                S.op("act", lambda e, b=b: e.activation(out=JK[0][:, 0:1].to_broadcast([128, 1024]), in_=X[b][:, :], func=AF.Square,
                                                        accum_out=SSX[:, b:b + 1]), R=[X[b]], W=[SSX, JK[0]])
            S.op("dve", lambda e: e.tensor_scalar(out=VX[:, c0:c1], in0=SSX[:, c0:c1], scalar1=1.0 / D, scalar2=EPS,
                                                  op0=ALU.mult, op1=ALU.add), R=[SSX], W=[VX])
            S.op("act", lambda e: e.activation(out=RSX[:, c0:c1], in_=VX[:, c0:c1], func=AF.Ln), R=[VX], W=[RSX])
            S.op("act", lambda e: e.activation(out=RSX[:, c0:c1], in_=RSX[:, c0:c1], func=AF.Exp, scale=-0.5), R=[RSX], W=[RSX])
            S.op("dve", lambda e: e.tensor_scalar(out=EPQ[:, c0:c1], in0=VX[:, c0:c1], scalar1=EPS, scalar2=None, op0=ALU.mult),
                 R=[VX], W=[EPQ])

        nfirst = 3 if len(blks) > 3 else len(blks)
        prepass(blks[:nfirst])
        chain["kt"], chain["v"] = None, None
        pend = None
        for bi, b in enumerate(blks):
            is_sample = (b == SB)
            if bi == 2 and nfirst < len(blks):
                prepass(blks[nfirst:])
            if bi == min(2, len(blks) - 1):
                cache_prep(l)
            midx = 2 if b == H else 0
            g_ = mixer_block(l, b, first=(bi == 0), is_sample=is_sample, mask_prev_idx=midx)
            ra = S.record(g_)
            rb = S.record(pend)
            S.emit_merged(ra, rb)
            pend = g_
        S.emit_merged([], S.record(pend))
        st["released"] = 22 * l + 6
        pump()
        S.barrier(MIXBUFS + list(XTALL.values()) + [ARENA])
        cur_pool[0] = "all"
        if l == 0:
            S.dma("act", ks[:, :, 0:WIN - DEC, :], ck[:, :, DEC:WIN, :])
            S.dma("act", vs[:, :, 0:WIN - DEC, :], cv[:, :, DEC:WIN, :])
        fb = [b for b in blks[1:]] if l < L - 1 else [b for b in blks if b >= H]
        S.op("dve", lambda e: e.tensor_scalar(out=VX[:, :], in0=SS2[:, :], scalar1=1.0 / D, scalar2=EPS, op0=ALU.mult, op1=ALU.add),
             R=[SS2], W=[VX])
        S.op("dve", lambda e: e.reciprocal(out=R2Q[:, :], in_=VX[:, :]), R=[VX], W=[R2Q])
        for c in range(4):
            gens = [ffn_block(l, c, b, last=(l == L - 1 and c == 3), fbi=fb.index(b)) for b in fb]
            n_ = len(gens)
            for i_ in range(n_ + 2):
                if i_ < n_:
                    next(gens[i_], None)
                if 0 <= i_ - 1 < n_:
                    next(gens[i_ - 1], None)
                if 0 <= i_ - 2 < n_:
                    next(gens[i_ - 2], None)
            st["released"] = 22 * l + 6 + 4 * (c + 1)
            if c < 3:
                pump()
    S.finish()
    return nc, es


def _consts(NB_pos):
    half = 8
    inv = np.power(np.float32(THETA), -2.0 * np.arange(half, dtype=np.float32) / 16.0).astype(np.float32)
    ang = NB_pos.astype(np.float32)[:, :, None] * inv[None, None, :]
    cs = np.concatenate([np.cos(ang), np.sin(ang)], axis=-1).astype(np.float32)
    return np.ascontiguousarray(cs.transpose(1, 0, 2).reshape(128, -1))


def _masks(first_core):
    j = np.arange(128)[:, None]
    i = np.arange(128)[None, :]
    prev = np.where(j >= i, 0.0, NEG)
    cur = np.where(j <= i, 0.0, NEG)
    first = np.full((128, 128), NEG) if first_core else prev
    scur = np.where((j // DEC == i // DEC) & (j % DEC <= i % DEC), 0.0, NEG)
    scache = np.where(j >= (i % DEC), 0.0, NEG)
    return np.ascontiguousarray(np.concatenate([prev, cur, first, scur, scache], axis=1).astype(np.float32))


def _tril2():
    j = np.arange(128)[:, None]
    i = np.arange(128)[None, :]
    a = (j <= i).astype(np.float32)
    b = ((j // DEC == i // DEC) & (j % DEC <= i % DEC)).astype(np.float32)
    return np.ascontiguousarray(np.concatenate([a, b], axis=1))


def _ws_sample(w_spatial, L):
    out = np.zeros((L, 128, NH, 128), np.float32)
    corner = w_spatial[:, :, :DEC, :DEC].transpose(0, 3, 1, 2)
    for q in range(NSEQ):
        out[:, q * DEC:(q + 1) * DEC, :, q * DEC:(q + 1) * DEC] = corner
    return np.ascontiguousarray(out.reshape(L, 128, NH * 128))


def make_in_maps(inp, L, NOWN, ncores, seg_per_batch):
    H = L
    f = lambda a: np.ascontiguousarray(np.asarray(a, dtype=np.float32))
    xp, xs = f(inp["x_prompt"]), f(inp["x_sample"])
    gT = np.concatenate([f(inp["g_mix"]).reshape(L, 8, 128).transpose(0, 2, 1),
                         np.concatenate([f(inp["g_out_a"]), f(inp["g_out_b"])], axis=1).reshape(L, 8, 128).transpose(0, 2, 1),
                         f(inp["g_ffn"]).reshape(L, 8, 128).transpose(0, 2, 1)], axis=2)
    shared = {
        "w_in": f(inp["w_in"]), "w_out": f(inp["w_out"]), "w_up": f(inp["w_up"]), "w_down": f(inp["w_down"]),
        "wsT": np.ascontiguousarray(f(inp["w_spatial"]).transpose(0, 3, 1, 2).reshape(L, 128, NH * 128)),
        "gT": np.ascontiguousarray(gT), "gsv": f(inp["g_sv"]),
        "gqk": np.ascontiguousarray(np.concatenate([f(inp["g_q"]), f(inp["g_k"])], axis=1)),
        "bsp": np.ascontiguousarray(f(inp["b_spatial"]).transpose(0, 2, 1)), "snk": f(inp["sinks"]),
        "tril": _tril2(), "identf": np.eye(128, dtype=np.float32),
        "wsTS": _ws_sample(f(inp["w_spatial"]), L),
        "bspS": np.ascontiguousarray(np.tile(f(inp["b_spatial"])[:, :, :DEC].transpose(0, 2, 1), (1, NSEQ, 1))),
    }
    maps = []
    for c in range(ncores):
        bidx, seg = c // seg_per_batch, c % seg_per_batch
        t0 = seg * NOWN * 128
        own = xp[bidx, t0:t0 + NOWN * 128].reshape(NOWN, 128, D)
        if seg > 0:
            halo = xp[bidx, t0 - H * 128:t0].reshape(H, 128, D)
        else:
            halo = np.zeros((H, 128, D), np.float32)
        samp = xs[c * NSEQ:(c + 1) * NSEQ].reshape(1, 128, D)
        pos = np.zeros((H + NOWN + 1, 128), np.float32)
        for j in range(H + NOWN):
            pos[j] = np.maximum(t0 - H * 128 + j * 128 + np.arange(128), 0)
        pos[H + NOWN] = PAST + (np.arange(128) % DEC)
        m = dict(shared)
        m["xin"] = np.ascontiguousarray(np.concatenate([halo, own, samp], axis=0))
        m["ck"] = np.ascontiguousarray(f(inp["cache_win_k"])[:, c * NSEQ:(c + 1) * NSEQ].reshape(L, NSEQ, WIN, 128))
        m["cv"] = np.ascontiguousarray(f(inp["cache_win_v"])[:, c * NSEQ:(c + 1) * NSEQ].reshape(L, NSEQ, WIN, 128))
        m["cs"] = _consts(pos)
        m["masks"] = _masks(seg == 0)
        maps.append(m)
    return maps


def assemble(res, L, NOWN, ncores, seg_per_batch, nbatch):
    SEQ = seg_per_batch * NOWN * 128
    yp = np.zeros((nbatch, SEQ, D), np.float32)
    ysm = np.zeros((ncores * NSEQ, DEC, D), np.float32)
    kpo = np.zeros((L, nbatch, WIN, KVH, HD), np.float32)
    vpo = np.zeros((L, nbatch, WIN, KVH, HD), np.float32)
    kso = np.zeros((L, ncores * NSEQ, WIN, KVH, HD), np.float32)
    vso = np.zeros((L, ncores * NSEQ, WIN, KVH, HD), np.float32)
    cvo = np.zeros((L, ncores * NSEQ, DEC, NH, HD), np.float32)
    for c in range(ncores):
        r = res[c]
        bidx, seg = c // seg_per_batch, c % seg_per_batch
        t0 = seg * NOWN * 128
        yp[bidx, t0:t0 + NOWN * 128] = r["y"][:NOWN].reshape(NOWN * 128, D)
        ysm[c * NSEQ:(c + 1) * NSEQ] = r["y"][NOWN].reshape(NSEQ, DEC, D)
        if seg == seg_per_batch - 1:
            kpo[:, bidx] = r["kp"].reshape(L, WIN, KVH, HD)
            vpo[:, bidx] = r["vp"].reshape(L, WIN, KVH, HD)
        kso[:, c * NSEQ:(c + 1) * NSEQ] = r["ks"].reshape(L, NSEQ, WIN, KVH, HD)
        vso[:, c * NSEQ:(c + 1) * NSEQ] = r["vs"].reshape(L, NSEQ, WIN, KVH, HD)
        cvo[:, c * NSEQ:(c + 1) * NSEQ] = r["cvs"].reshape(L, NSEQ, DEC, NH, HD)
    return yp, ysm, kpo, vpo, kso, vso, cvo


def kernel(**inputs):
    L, NOWN, NC_, SPB = 4, 16, 8, 4
    nc, es = build(L, NOWN)
    maps = make_in_maps(inputs, L, NOWN, NC_, SPB)
    res = run_bass_kernel_spmd(nc, maps, core_ids=list(range(NC_)))
    es.close()
    return assemble(res.results, L, NOWN, NC_, SPB, 2)
```
